# Optimizing a Trainium2 kernel written in Bass

```python
import math
import jax, jax.numpy as jnp
from jax import lax
import numpy as np


D_MODEL = 1024
BATCH = 16
SEQ = 256
DEPTH = 2
DEC_BATCH = 4
DEC_SEQ = 2048
PAST_LEN = 512

GRID_W = 64
N_EVEN = (DEPTH + 1) // 2
N_ODD = DEPTH // 2
DA_HEADS = 4
DA_QK = 64
DA_V = 2 * DA_QK
DA_WIDTH = DA_HEADS * DA_V
S5_WIDTH = D_MODEL - DA_WIDTH
S5_CH = 16
S5_GROUPS = S5_WIDTH // S5_CH
S5_STATE = 64
IN_WIDTH = 3 * DA_WIDTH + S5_WIDTH
POOL_WINDOWS = (2, 4, 8, 16)
POOL_GROUPS = len(POOL_WINDOWS)
POOL_CH = D_MODEL // POOL_GROUPS
FF_HIDDEN = 4 * D_MODEL
N_MOD = 6
ROPE_BASE = 10000.0
Q_BLOCK = 128
EPS = 1e-6

kernel_name = 'hybrid_diffattn_s5_pool_diffusion_step'


def rmsnorm(x, g):
    xf = x.astype(jnp.float32)
    y = xf * lax.rsqrt(jnp.mean(xf * xf, axis=-1, keepdims=True) + EPS)
    return y.astype(x.dtype) * g


def modulate(h, shift, scale):
    return h * (1 + scale) + shift


def axial_angles(L):
    rows = L // GRID_W
    row = jnp.repeat(jnp.arange(rows), GRID_W).astype(jnp.float32)
    col = jnp.tile(jnp.arange(GRID_W), rows).astype(jnp.float32)
    half = DA_QK // 2
    inv = 1.0 / (ROPE_BASE ** (jnp.arange(0, half, 2, dtype=jnp.float32) / half))
    return row[:, None] * inv, col[:, None] * inv


def rotate(seg, ang):
    cos = jnp.cos(ang)[None, :, None, None, :].astype(seg.dtype)
    sin = jnp.sin(ang)[None, :, None, None, :].astype(seg.dtype)
    x1, x2 = jnp.split(seg, 2, axis=-1)
    return jnp.concatenate([x1 * cos - x2 * sin, x2 * cos + x1 * sin], axis=-1)


def axial_rope(x):
    ang_r, ang_c = axial_angles(x.shape[1])
    half = DA_QK // 2
    return jnp.concatenate([rotate(x[..., :half], ang_r), rotate(x[..., half:], ang_c)], axis=-1)


def split_in(h, w_in):
    B, L, _ = h.shape
    z = h @ w_in
    q = z[..., :DA_WIDTH].reshape(B, L, DA_HEADS, 2, DA_QK)
    k = z[..., DA_WIDTH:2 * DA_WIDTH].reshape(B, L, DA_HEADS, 2, DA_QK)
    v = z[..., 2 * DA_WIDTH:3 * DA_WIDTH].reshape(B, L, DA_HEADS, DA_V)
    u = z[..., 3 * DA_WIDTH:]
    return q, k, v, u


def diff_lambda(lambda_qk, lam_init):
    lq1, lk1, lq2, lk2 = lambda_qk.astype(jnp.float32)
    return jnp.exp(jnp.sum(lq1 * lk1)) - jnp.exp(jnp.sum(lq2 * lk2)) + lam_init


def diff_attention(q, k, v, lam):
    B, Lq = q.shape[0], q.shape[1]
    nblk = Lq // Q_BLOCK
    scale = DA_QK ** -0.5
    qb = q.reshape(B, nblk, Q_BLOCK, DA_HEADS, 2, DA_QK).transpose(1, 0, 2, 3, 4, 5)

    def one_block(qblk):
        s = jnp.einsum('bqhmd,bkhmd->bhmqk', qblk, k).astype(jnp.float32) * scale
        p = jax.nn.softmax(s, axis=-1)
        w = p[:, :, 0] - lam * p[:, :, 1]
        return jnp.einsum('bhqk,bkhe->bqhe', w.astype(v.dtype), v)

    out = lax.map(one_block, qb)
    return out.transpose(1, 0, 2, 3, 4).reshape(B, Lq, DA_HEADS, DA_V)


def diff_head_out(o, subln_g, lam_init):
    B, L = o.shape[0], o.shape[1]
    o = rmsnorm(o, subln_g) * (1.0 - lam_init)
    return o.reshape(B, L, DA_WIDTH)


def s5_discretize(lam_re, lam_im, log_dt, b_re, b_im):
    lam_re = lam_re.astype(jnp.float32)
    lam_im = lam_im.astype(jnp.float32)
    dt = jnp.exp(log_dt.astype(jnp.float32))[:, None]
    mag = jnp.exp(lam_re * dt)
    ang = lam_im * dt
    a_re = mag * jnp.cos(ang)
    a_im = mag * jnp.sin(ang)
    den = lam_re * lam_re + lam_im * lam_im
    f_re = ((a_re - 1.0) * lam_re + a_im * lam_im) / den
    f_im = (a_im * lam_re - (a_re - 1.0) * lam_im) / den
    b_re = b_re.astype(jnp.float32)
    b_im = b_im.astype(jnp.float32)
    bb_re = f_re[..., None] * b_re - f_im[..., None] * b_im
    bb_im = f_re[..., None] * b_im + f_im[..., None] * b_re
    return a_re, a_im, bb_re, bb_im


def complex_affine_combine(e1, e2):
    a1r, a1i, b1r, b1i = e1
    a2r, a2i, b2r, b2i = e2
    return (a2r * a1r - a2i * a1i,
            a2r * a1i + a2i * a1r,
            a2r * b1r - a2i * b1i + b2r,
            a2r * b1i + a2i * b1r + b2i)


def s5_direction(uf, lam_re, lam_im, log_dt, b_re, b_im, c_re, c_im, h0, reverse):
    a_re, a_im, bb_re, bb_im = s5_discretize(lam_re, lam_im, log_dt, b_re, b_im)
    bu_re = jnp.einsum('blgh,gph->blgp', uf, bb_re)
    bu_im = jnp.einsum('blgh,gph->blgp', uf, bb_im)
    if h0 is not None:
        h0_re, h0_im = h0
        pos = -1 if reverse else 0
        bu_re = bu_re.at[:, pos].add(a_re * h0_re - a_im * h0_im)
        bu_im = bu_im.at[:, pos].add(a_re * h0_im + a_im * h0_re)
    elems = (jnp.broadcast_to(a_re, bu_re.shape), jnp.broadcast_to(a_im, bu_re.shape), bu_re, bu_im)
    _, _, h_re, h_im = lax.associative_scan(complex_affine_combine, elems, reverse=reverse, axis=1)
    y = (jnp.einsum('blgp,ghp->blgh', h_re, c_re.astype(jnp.float32))
         - jnp.einsum('blgp,ghp->blgh', h_im, c_im.astype(jnp.float32)))
    last = 0 if reverse else -1
    return y, h_re[:, last], h_im[:, last]


def s5_mixer(u, lam_re, lam_im, log_dt, b_re, b_im, c_re, c_im, d_skip, w_glu, h0):
    B, L, _ = u.shape
    uf = u.astype(jnp.float32).reshape(B, L, S5_GROUPS, S5_CH)
    ys, finals = [], []
    for dr in (0, 1):
        init = None if h0 is None else (h0[:, dr, :, :, 0].astype(jnp.float32), h0[:, dr, :, :, 1].astype(jnp.float32))
        y, fr, fi = s5_direction(uf, lam_re[dr], lam_im[dr], log_dt[dr], b_re[dr], b_im[dr],
                                 c_re[dr], c_im[dr], init, dr == 1)
        ys.append(y)
        finals.append(jnp.stack([fr, fi], axis=-1))
    y = (ys[0] + ys[1]).reshape(B, L, S5_WIDTH) + d_skip.astype(jnp.float32) * u.astype(jnp.float32)
    g = jax.nn.gelu(y).astype(u.dtype)
    out = g * jax.nn.sigmoid(g @ w_glu)
    state = jnp.stack(finals, axis=1).astype(u.dtype) if h0 is None else None
    return out, state


def pool_mixer(h, pool_w, pool_scale):
    B, L, D = h.shape
    hf = h.astype(jnp.float32)
    csum = jnp.concatenate([jnp.zeros((B, 1, D), jnp.float32), jnp.cumsum(hf, axis=1)], axis=1)
    t = jnp.arange(L)
    outs = []
    for g, w in enumerate(POOL_WINDOWS):
        lo = jnp.clip(t - w // 2, 0, L)
        hi = jnp.clip(t + w // 2, 0, L)
        seg = csum[..., g * POOL_CH:(g + 1) * POOL_CH]
        mean = (seg[:, hi] - seg[:, lo]) / (hi - lo).astype(jnp.float32)[None, :, None]
        outs.append(mean - hf[..., g * POOL_CH:(g + 1) * POOL_CH])
    z = jnp.stack(outs, axis=2).astype(h.dtype)
    z = jnp.einsum('blgc,gcd->blgd', z, pool_w).reshape(B, L, D)
    return z * pool_scale


def sq_relu_mlp(h, w1, w2):
    return jnp.square(jax.nn.relu(h @ w1)) @ w2


def setup_inputs(seed: int = 0) -> dict:
    key = jax.random.key(seed)
    ks = jax.random.split(key, 32)
    nrm = jax.random.normal
    f32 = jnp.float32
    lam_im_base = jnp.pi * jnp.arange(S5_STATE, dtype=f32)
    return {
        'x_prompt': nrm(ks[0], (BATCH, SEQ, D_MODEL), f32),
        'x_sample': nrm(ks[1], (DEC_BATCH, DEC_SEQ, D_MODEL), f32),
        'cache_k': nrm(ks[2], (DEC_BATCH, N_EVEN, PAST_LEN, DA_HEADS, 2 * DA_QK), f32),
        'cache_v': nrm(ks[3], (DEC_BATCH, N_EVEN, PAST_LEN, DA_HEADS, DA_V), f32),
        'state_s5': 0.5 * nrm(ks[4], (DEC_BATCH, N_EVEN, 2, S5_GROUPS, S5_STATE, 2), f32),
        'c': nrm(ks[5], (DEC_BATCH, D_MODEL), f32),
        'c_ctx': nrm(ks[6], (D_MODEL,), f32),
        'mod_w': 0.5 * D_MODEL ** -0.5 * nrm(ks[7], (DEPTH, D_MODEL, N_MOD * D_MODEL), f32),
        'mod_b': 0.01 * nrm(ks[8], (DEPTH, N_MOD * D_MODEL), f32),
        'norm_g': 1.0 + 0.02 * nrm(ks[9], (DEPTH, 2, D_MODEL), f32),
        'mix_w_in': D_MODEL ** -0.5 * nrm(ks[10], (N_EVEN, D_MODEL, IN_WIDTH), f32),
        'mix_w_out': D_MODEL ** -0.5 * nrm(ks[11], (N_EVEN, D_MODEL, D_MODEL), f32),
        'diff_lambda_qk': 0.1 * nrm(ks[12], (N_EVEN, 4, DA_QK), f32),
        'diff_subln_g': 1.0 + 0.02 * nrm(ks[13], (N_EVEN, DA_V), f32),
        's5_lambda_re': -0.5 + 0.01 * nrm(ks[14], (N_EVEN, 2, S5_GROUPS, S5_STATE), f32),
        's5_lambda_im': lam_im_base + 0.01 * nrm(ks[15], (N_EVEN, 2, S5_GROUPS, S5_STATE), f32),
        's5_log_dt': jax.random.uniform(ks[16], (N_EVEN, 2, S5_GROUPS), f32, math.log(1e-3), math.log(1e-1)),
        's5_b_re': (2 * S5_CH) ** -0.5 * nrm(ks[17], (N_EVEN, 2, S5_GROUPS, S5_STATE, S5_CH), f32),
        's5_b_im': (2 * S5_CH) ** -0.5 * nrm(ks[18], (N_EVEN, 2, S5_GROUPS, S5_STATE, S5_CH), f32),
        's5_c_re': (2 * S5_STATE) ** -0.5 * nrm(ks[19], (N_EVEN, 2, S5_GROUPS, S5_CH, S5_STATE), f32),
        's5_c_im': (2 * S5_STATE) ** -0.5 * nrm(ks[20], (N_EVEN, 2, S5_GROUPS, S5_CH, S5_STATE), f32),
        's5_d': nrm(ks[21], (N_EVEN, S5_WIDTH), f32),
        's5_w_glu': S5_WIDTH ** -0.5 * nrm(ks[22], (N_EVEN, S5_WIDTH, S5_WIDTH), f32),
        'pool_w': POOL_CH ** -0.5 * nrm(ks[23], (N_ODD, POOL_GROUPS, POOL_CH, POOL_CH), f32),
        'pool_scale': 1.0 + 0.02 * nrm(ks[24], (N_ODD, D_MODEL), f32),
        'ff_w1': D_MODEL ** -0.5 * nrm(ks[25], (DEPTH, D_MODEL, FF_HIDDEN), f32),
        'ff_w2': FF_HIDDEN ** -0.5 * nrm(ks[26], (DEPTH, FF_HIDDEN, D_MODEL), f32),
        'final_norm_g': 1.0 + 0.02 * nrm(ks[27], (D_MODEL,), f32),
    }


def reference(x_prompt, x_sample, cache_k, cache_v, state_s5, c, c_ctx, mod_w, mod_b, norm_g,
              mix_w_in, mix_w_out, diff_lambda_qk, diff_subln_g, s5_lambda_re, s5_lambda_im, s5_log_dt,
              s5_b_re, s5_b_im, s5_c_re, s5_c_im, s5_d, s5_w_glu, pool_w, pool_scale, ff_w1, ff_w2,
              final_norm_g):
    ctx = x_prompt
    lat = x_sample
    new_k, new_v, new_s = [], [], []
    for i in range(DEPTH):
        m_ctx = (jax.nn.silu(c_ctx) @ mod_w[i] + mod_b[i]).reshape(N_MOD, D_MODEL)
        m_lat = (jax.nn.silu(c) @ mod_w[i] + mod_b[i]).reshape(-1, N_MOD, 1, D_MODEL)
        hc = modulate(rmsnorm(ctx, norm_g[i, 0]), m_ctx[0], m_ctx[1])
        hl = modulate(rmsnorm(lat, norm_g[i, 0]), m_lat[:, 0], m_lat[:, 1])
        if i % 2 == 0:
            j = i // 2
            lam_init = 0.8 - 0.6 * math.exp(-0.3 * i)
            lam = diff_lambda(diff_lambda_qk[j], lam_init)
            s5p = (s5_lambda_re[j], s5_lambda_im[j], s5_log_dt[j], s5_b_re[j], s5_b_im[j],
                   s5_c_re[j], s5_c_im[j], s5_d[j], s5_w_glu[j])
            qc, kc, vc, uc = split_in(hc, mix_w_in[j])
            ac = diff_attention(qc, kc, vc, lam)
            sc, st_c = s5_mixer(uc, *s5p, None)
            oc = jnp.concatenate([diff_head_out(ac, diff_subln_g[j], lam_init), sc], axis=-1) @ mix_w_out[j]
            new_k.append(kc.reshape(kc.shape[0], kc.shape[1], DA_HEADS, 2 * DA_QK))
            new_v.append(vc)
            new_s.append(st_c)
            ql, kl, vl, ul = split_in(hl, mix_w_in[j])
            ql, kl = axial_rope(ql), axial_rope(kl)
            ck = cache_k[:, j].reshape(cache_k.shape[0], cache_k.shape[2], DA_HEADS, 2, DA_QK)
            k_all = jnp.concatenate([kl, ck], axis=1)
            v_all = jnp.concatenate([vl, cache_v[:, j]], axis=1)
            al = diff_attention(ql, k_all, v_all, lam)
            sl, _ = s5_mixer(ul, *s5p, state_s5[:, j])
            ol = jnp.concatenate([diff_head_out(al, diff_subln_g[j], lam_init), sl], axis=-1) @ mix_w_out[j]
        else:
            j = i // 2
            oc = pool_mixer(hc, pool_w[j], pool_scale[j])
            ol = pool_mixer(hl, pool_w[j], pool_scale[j])
        ctx = ctx + m_ctx[2] * oc
        lat = lat + m_lat[:, 2] * ol
        hc = modulate(rmsnorm(ctx, norm_g[i, 1]), m_ctx[3], m_ctx[4])
        hl = modulate(rmsnorm(lat, norm_g[i, 1]), m_lat[:, 3], m_lat[:, 4])
        ctx = ctx + m_ctx[5] * sq_relu_mlp(hc, ff_w1[i], ff_w2[i])
        lat = lat + m_lat[:, 5] * sq_relu_mlp(hl, ff_w1[i], ff_w2[i])
    y_prompt = rmsnorm(ctx, final_norm_g)
    y_sample = rmsnorm(lat, final_norm_g)
    new_cache_k = jnp.stack(new_k, axis=1)
    new_cache_v = jnp.stack(new_v, axis=1)
    new_state_s5 = jnp.stack(new_s, axis=1)
    return (y_prompt, y_sample, new_cache_k, new_cache_v, new_state_s5)
```

```python
import math
import os
import numpy as np
import concourse.bass as bass
import concourse.mybir as mybir
from concourse.bass_utils import run_bass_kernel_spmd

F32 = mybir.dt.float32
BF16 = mybir.dt.bfloat16
ALU = mybir.AluOpType
AF = mybir.ActivationFunctionType

ENGS = ("pe", "act", "dve", "pool", "sp")
NPOOL = 12


DEBUG_TAGS = None


def _tagged(fn, tag):
    def g(e):
        ins = fn(e)
        DEBUG_TAGS[ins.ins.name] = tag
        return ins
    return g


class Buf:
    __slots__ = ("name", "w", "r")

    def __init__(self, name=""):
        self.name = name
        self.w = None
        self.r = []


class Prog:
    def __init__(self, nc):
        self.nc = nc
        self.ops = {e: [] for e in ENGS}
        self.cnt = {e: 0 for e in ("pe", "act", "dve", "pool")}
        self.sem = {}
        for e in ("pe", "act", "dve", "pool"):
            self.sem[e] = nc.alloc_semaphore(name="s_" + e)
        self.dq = {}
        for q in ("sp", "pool", "act"):
            self.dq[q] = 0
            for i in range(NPOOL):
                self.sem[("d", q, i)] = nc.alloc_semaphore(name=f"d_{q}{i}")
        self.known = {e: {} for e in ENGS}

    def _collect(self, eng, reads, writes):
        waits = {}

        def need(ev):
            if ev is None:
                return
            k, v, pe = ev
            if eng == "pe" and pe == "pe":
                return
            if waits.get(k, 0) < v:
                waits[k] = v
        for b in reads:
            need(b.w)
        for b in writes:
            need(b.w)
            for ev in b.r:
                need(ev)
        out = []
        kn = self.known[eng]
        for k, v in waits.items():
            if kn.get(k, 0) >= v:
                continue
            kn[k] = v
            out.append((k, v))
        return out

    def _record(self, ev, reads, writes):
        for b in writes:
            b.w = ev
            b.r = []
        for b in reads:
            b.r = [e for e in b.r if e[0] != ev[0]] + [ev]

    def op(self, eng, fn, reads=(), writes=()):
        if DEBUG_TAGS is not None:
            import traceback
            st_ = traceback.extract_stack(limit=4)
            fn = _tagged(fn, " <- ".join(f"{f.lineno}" for f in st_[:-1]))
        waits = self._collect(eng, reads, writes)
        self.cnt[eng] += 1
        ev = (eng, self.cnt[eng], eng)
        self._record(ev, reads, writes)
        self.ops[eng].append((waits, fn, (eng, 1)))
        return ev

    def dma(self, q, fn, reads=(), writes=()):
        n = self.dq[q]
        self.dq[q] += 1
        slot, rnd = n % NPOOL, n // NPOOL
        key = ("d", q, slot)
        waits = self._collect(q, reads, writes)
        if rnd > 0:
            kn = self.known[q]
            if kn.get(key, 0) < 16 * rnd:
                kn[key] = 16 * rnd
                waits.append((key, 16 * rnd))
        ev = (key, 16 * (rnd + 1), "dma")
        self._record(ev, reads, writes)
        self.ops[q].append((waits, fn, (key, 16)))
        return ev

    def barrier(self):
        evs = [(e, self.cnt[e], e) for e in self.cnt if self.cnt[e] > 0]
        for q in self.dq:
            n = self.dq[q]
            for slot in range(min(n, NPOOL)):
                last = ((n - 1 - slot) // NPOOL) * NPOOL + slot
                rnd = last // NPOOL
                evs.append((("d", q, slot), 16 * (rnd + 1), "dma"))
        for e in ENGS:
            self.wait_all_on(e, evs)

    def wait_all_on(self, eng, events):
        waits = []
        kn = self.known[eng]
        for k, v, _ in events:
            if kn.get(k, 0) < v:
                kn[k] = v
                waits.append((k, v))
        self.ops[eng].append((waits, None, None))

    def emit(self):
        nc = self.nc
        engobj = {"pe": "tensor", "act": "scalar", "dve": "vector", "pool": "gpsimd", "sp": "sync"}
        with nc.Block() as block:
            for e in ENGS:
                ops = self.ops[e]
                if not ops:
                    continue

                def body(eng, ops=ops):
                    for waits, fn, inc in ops:
                        for k, v in waits:
                            eng.wait_ge(self.sem[k], v)
                        if fn is not None:
                            fn(eng).then_inc(self.sem[inc[0]], inc[1])
                getattr(block, engobj[e])(body)


D = 1024
NT = 2560
NB = 5
TB = 512
T = 16
NCL = 2048 // T
NCC = 512 // T
EPS = 1e-6
LAM_INIT0 = 0.8 - 0.6 * math.exp(0.0)
GC = 1.5957691216057308


def AP(t, off, dims, p0=0, p1=128):
    base = t[p0:p1]
    return bass.AP(base.tensor, base.offset + off, [list(base.ap[0])] + [list(d) for d in dims])


def DAP(t, off, dims):
    return bass.AP(t, off, [list(d) for d in dims])


STOP = 99
SUB = 99
SUBB = 99


def build():
    ph = lambda n: n <= STOP
    nc = bass.Bass("TRN2", target_bir_lowering=False)
    P = Prog(nc)

    def din(name, shape):
        return nc.dram_tensor(name, list(shape), F32, kind="ExternalInput")

    def dout(name, shape):
        return nc.dram_tensor(name, list(shape), F32, kind="ExternalOutput")

    xc = din("xc", [512, D]); xl = din("xl", [2048, D])
    ck = din("ck", [512, 512]); cv = din("cv", [512, 512])
    st = din("st", [2, 32, 64, 2]); cvec = din("cvec", [2, D])
    mod_w = din("mod_w", [2, D, 6 * D]); mod_b = din("mod_b", [2, 6 * D]); norm_g = din("norm_g", [2, 2, D])
    w_in = din("w_in", [D, 2048]); w_out = din("w_out", [D, D])
    lam_qk = din("lam_qk", [4, 64]); subln = din("subln", [128])
    lam_re = din("lam_re", [2, 32, 64]); lam_im = din("lam_im", [2, 32, 64]); log_dt = din("log_dt", [2, 32])
    b_re = din("b_re", [2, 32, 64, 16]); b_im = din("b_im", [2, 32, 64, 16])
    c_re = din("c_re", [2, 32, 16, 64]); c_im = din("c_im", [2, 32, 16, 64])
    s5_d = din("s5_d", [512]); w_glu = din("w_glu", [512, 512])
    pool_w = din("pool_w", [4, 256, 256]); pool_scale = din("pool_scale", [D])
    ff_w1 = din("ff_w1", [2, D, 4 * D]); ff_w2 = din("ff_w2", [2, 4 * D, D]); fng = din("fng", [D])
    c_ident = din("c_ident", [128, 128]); c_maskq = din("c_maskq", [128, 128])
    c_posr = din("c_posr", [2048]); c_posc = din("c_posc", [2048]); c_fi = din("c_fi", [128, 2])

    yc = dout("yc", [512, D]); yl = dout("yl", [2048, D])
    nk = dout("nk", [512, 512]); nv = dout("nv", [512, 512]); ns = dout("ns", [2, 2, 32, 64, 2])

    def dscr(name, shape):
        return nc.dram_tensor(name, list(shape), BF16, kind="Internal")
    QT = dscr("QT", [4, 128, NT]); KT = dscr("KT", [4, 128, 3072]); VS = dscr("VS", [3072, 512])
    UT = dscr("UT", [4, 128, NT]); MO = dscr("MO", [8, 128, NT]); GT = dscr("GT", [4, 128, NT])
    DBG = nc.dram_tensor("DBG", [512, 512], F32, kind="Internal")
    bQT, bKT, bVS, bUT, bMO, bGT = [Buf(n) for n in ("QT", "KT", "VS", "UT", "MO", "GT")]
    bQTb = [Buf() for _ in range(NB)]; bUTb = [Buf() for _ in range(NB)]; bMOb = [Buf() for _ in range(NB)]
    out_events = []

    XR = nc.alloc_sbuf_tensor("XR", [128, 8, NT], F32)
    H = nc.alloc_sbuf_tensor("H", [128, 8, NT], BF16)
    bXR = [Buf(f"XR{b}") for b in range(NB)]
    bH = [Buf(f"H{b}") for b in range(NB)]
    ARN = 20480
    AR = nc.alloc_sbuf_tensor("AR", [128, ARN], F32)
    ident = nc.alloc_sbuf_tensor("ident", [128, 128], F32)
    identb = nc.alloc_sbuf_tensor("identb", [128, 128], BF16)
    onesb = nc.alloc_sbuf_tensor("onesb", [128, 128], BF16)
    maskq = nc.alloc_sbuf_tensor("maskq", [128, 128], F32)
    MODT = nc.alloc_sbuf_tensor("MODT", [128, 2, 48, 2], F32)
    NG = nc.alloc_sbuf_tensor("NG", [128, 5, 8], F32)
    GS = nc.alloc_sbuf_tensor("GS", [128, 2, 2, 2, 8, 2], F32)
    GATE = nc.alloc_sbuf_tensor("GATE", [128, 2, 2, 8, 2], F32)
    SMALL = nc.alloc_sbuf_tensor("SMALL", [128, 64], F32)
    RSTD = nc.alloc_sbuf_tensor("RSTD", [128, TB], F32)
    bC = Buf("consts"); bMODs = [Buf("mod0"), Buf("mod1")]; bSMALL = Buf("small"); bRSTD = Buf("rstd")

    PS = [nc.alloc_psum_tensor(f"ps{i}", [128, 512], F32) for i in range(8)]
    bPS = [Buf(f"ps{i}") for i in range(8)]

    def arf(off, n):
        return AR[:, off:off + n]

    def arb(off, n):
        return AR[:, off:off + n].bitcast(BF16)

    def act(out, in_, func, R, W, **kw):
        return P.op("act", lambda e: e.activation(out=out, in_=in_, func=func, **kw), reads=R, writes=W)

    def tt(eng, out, a, b, op, R, W):
        return P.op(eng, lambda e: e.tensor_tensor(out=out, in0=a, in1=b, op=op), reads=R, writes=W)

    def ts(eng, out, a, s1, s2, op0, op1, R, W):
        if s2 is None:
            return P.op(eng, lambda e: e.tensor_scalar(out=out, in0=a, scalar1=s1, scalar2=None, op0=op0), reads=R, writes=W)
        return P.op(eng, lambda e: e.tensor_scalar(out=out, in0=a, scalar1=s1, scalar2=s2, op0=op0, op1=op1), reads=R, writes=W)

    def stt(eng, out, a, s, b, op0, op1, R, W):
        return P.op(eng, lambda e: e.scalar_tensor_tensor(out=out, in0=a, scalar=s, in1=b, op0=op0, op1=op1), reads=R, writes=W)

    def cp(eng, out, in_, R, W):
        return P.op(eng, lambda e: e.tensor_copy(out=out, in_=in_), reads=R, writes=W)

    def ms(eng, out, val, W):
        return P.op(eng, lambda e: e.memset(out, val), writes=W)

    def mm(out, lhsT, rhs, start, stop, R, W, **kw):
        return P.op("pe", lambda e: e.matmul(out, lhsT=lhsT, rhs=rhs, start=start, stop=stop, **kw), reads=R, writes=W)

    def tr(out, in_, idn, R, W):
        return P.op("pe", lambda e: e.transpose(out, in_, idn), reads=R, writes=W)

    def ld(q, out, in_, R, W, **kw):
        return P.dma(q, lambda e: e.dma_start(out=out, in_=in_, **kw), reads=R, writes=W)

    SL = dict(allow_slow_non_contiguous=True)

    ld("sp", ident[:], c_ident.ap(), [], [bC])
    ld("sp", maskq[:], c_maskq.ap(), [], [bC])
    cp("dve", identb[:], ident[:], [bC], [bC])
    ms("dve", onesb[:], 1.0, [bC])
    for i in range(4):
        ld("sp", NG[:, i, :], DAP(norm_g, i * D, [[1, 128], [128, 8]]), [], [bC], **SL)
    ld("sp", NG[:, 4, :], DAP(fng, 0, [[1, 128], [128, 8]]), [], [bC], **SL)

    WI_pre = arb(0, 8192)
    bWI_pre = Buf("wi")
    ld("pool", WI_pre.rearrange("p (k c) -> p k c", c=2048), DAP(w_in, 0, [[2048, 128], [128 * 2048, 8], [1, 2048]]), [], [bWI_pre])
    XT = [arf(8192 + i * 1024, 1024) for i in range(4)]
    bXT = [Buf() for _ in range(4)]
    for ti in range(20):
        src = xc if ti < 4 else xl
        r0 = ti * 128 if ti < 4 else (ti - 4) * 128
        b = ti // 4
        xt = XT[ti % 4]; bx = bXT[ti % 4]
        ld("sp", xt, DAP(src, r0 * D, [[D, 128], [1, D]]), [], [bx])
        pa, pb = (4, 5) if ti % 2 == 0 else (6, 7)
        for k in range(8):
            pst = PS[pa if k < 4 else pb]; bp = bPS[pa if k < 4 else pb]
            tr(pst[:, (k % 4) * 128:(k % 4 + 1) * 128], xt[:, k * 128:(k + 1) * 128], ident[:], [bx, bC], [bp])
        for half in range(2):
            pst = PS[pa if half == 0 else pb]; bp = bPS[pa if half == 0 else pb]
            o = XR[:, half * 4:half * 4 + 4, ti * 128:(ti + 1) * 128]
            i_ = pst[:, :].rearrange("p (k t) -> p k t", t=128)
            if half == 0:
                cp("dve", o, i_, [bp], [bXR[b]])
            else:
                act(o, i_, AF.Copy, [bp], [bXR[b]])

    SCf = nc.alloc_sbuf_tensor("SCf", [128, 16], F32)
    SCbT = nc.alloc_sbuf_tensor("SCb", [128, 16], BF16)
    SCb = SCbT[:, :]
    MBT = nc.alloc_sbuf_tensor("MBT", [128, 2, 48], F32)
    bSC = Buf("sc"); bMB = Buf("mb")
    for s_ in range(2):
        ld("sp", SCf[:, :].rearrange("p (k s) -> p k s", s=2)[:, :, s_], DAP(cvec, s_ * D, [[1, 128], [128, 8]]), [], [bSC], **SL)
    act(SCb, SCf[:, :], AF.Silu, [bSC], [bSC])
    for l in range(2):
        ld("sp", MBT[:, l, :], DAP(mod_b, l * 6 * D, [[1, 128], [128, 48]]), [], [bMB], **SL)
    mci = [0]

    def mod_chunk(l, cc, MW, bMW, bank=0, per_chunk=False):
        w = MW[mci[0] % 2]; bw = bMW[mci[0] % 2]; mci[0] += 1
        ld("pool", w.rearrange("p (k c) -> p k c", c=512),
           DAP(mod_w, l * D * 6 * D + cc * 512, [[6 * D, 128], [128 * 6 * D, 8], [1, 512]]), [], [bw])
        for t4 in range(4):
            tt_i = cc * 4 + t4
            c0 = (t4 * 2) if per_chunk else (tt_i * 2)
            for k in range(8):
                mm(PS[bank][:, c0:c0 + 2], w[:, k * 512 + t4 * 128:k * 512 + (t4 + 1) * 128],
                   SCb[:, k * 2:k * 2 + 2], k == 0, k == 7, [bw, bSC], [bPS[bank]])
        if per_chunk:
            tt("dve", MODT[:, l, cc * 4:(cc + 1) * 4, :], PS[bank][:, 0:8].rearrange("p (t s) -> p t s", s=2),
               MBT[:, l, cc * 4:(cc + 1) * 4].unsqueeze(2).to_broadcast([128, 4, 2]), ALU.add, [bPS[bank], bMB], [bMODs[l]])

    def mod_finish(l, evac=True):
        if evac:
            tt("dve", MODT[:, l, :, :], PS[0][:, 0:96].rearrange("p (t s) -> p t s", s=2),
               MBT[:, l, :].unsqueeze(2).to_broadcast([128, 48, 2]), ALU.add, [bPS[0], bMB], [bMODs[l]])
        for n in range(2):
            sh = MODT[:, l, (3 * n) * 8:(3 * n) * 8 + 8, :]
            sc = MODT[:, l, (3 * n + 1) * 8:(3 * n + 1) * 8 + 8, :]
            gt = MODT[:, l, (3 * n + 2) * 8:(3 * n + 2) * 8 + 8, :]
            g = NG[:, l * 2 + n, :].unsqueeze(2).to_broadcast([128, 8, 2])
            stt("dve", GS[:, l, n, 0, :, :], sc, 1.0, g, ALU.add, ALU.mult, [bMODs[l], bC], [bMODs[l]])
            cp("dve", GS[:, l, n, 1, :, :], sh, [bMODs[l]], [bMODs[l]])
            cp("dve", GATE[:, l, n, :, :], gt, [bMODs[l]], [bMODs[l]])
    MW0 = [arb(12288 + i * 2048, 2048) for i in range(2)]
    bMW0 = [Buf("mw0"), Buf("mw1")]
    for cc in range(12):
        mod_chunk(0, cc, MW0, bMW0)
    mod_finish(0)

    SQ = arb(ARN - 2048, 2048)
    TMPNS = [arf(ARN - 2048 - 512, 512), arf(ARN - 2048 - 1024, 512)]
    bSQ = Buf("sq"); bTMPNS = [Buf("tmpn0"), Buf("tmpn1")]
    tni = [0]

    def norm_block(b, gs_fn, sh_fn, out_fn, outbufs, psb=7, bMOD=bC):
        tb = slice(b * TB, (b + 1) * TB)
        for k in range(8):
            if k % 2 == 0:
                act(SQ[:, k * 512:(k + 1) * 512], XR[:, k, tb], AF.Square, [bXR[b]], [bSQ])
            else:
                tt("dve", SQ[:, k * 512:(k + 1) * 512], XR[:, k, tb], XR[:, k, tb], ALU.mult, [bXR[b]], [bSQ])
        for k in range(8):
            mm(PS[psb][:, :], onesb[:], SQ[:, k * 512:(k + 1) * 512], k == 0, k == 7, [bSQ, bC], [bPS[psb]])
        act(RSTD[:], PS[psb][:, :], AF.Ln, [bPS[psb]], [bRSTD], scale=1.0 / D, bias=EPS)
        act(RSTD[:], RSTD[:], AF.Exp, [bRSTD], [bRSTD], scale=-0.5)
        for k in range(8):
            sh = sh_fn(k)
            if sh is None:
                stt("dve", out_fn(k), XR[:, k, tb], gs_fn(k), RSTD[:], ALU.mult, ALU.mult, [bXR[b], bRSTD, bMOD, bC], outbufs)
            else:
                TMPN = TMPNS[tni[0] % 2]; bTMPN = bTMPNS[tni[0] % 2]; tni[0] += 1
                stt("dve", TMPN, XR[:, k, tb], gs_fn(k), RSTD[:], ALU.mult, ALU.mult, [bXR[b], bRSTD, bMOD], [bTMPN])
                act(out_fn(k), TMPN, AF.Identity, [bTMPN, bMOD], outbufs, bias=sh, scale=1.0)

    def norm_to_H(l, n):
        for b in range(NB):
            s = 0 if b == 0 else 1
            norm_block(b, lambda k: GS[:, l, n, 0, k, s:s + 1], lambda k: GS[:, l, n, 1, k, s:s + 1],
                       lambda k: H[:, k, b * TB:(b + 1) * TB], [bH[b]], bMOD=bMODs[l])

    def mlp(l):
        W1 = [arb(i * 2048, 2048) for i in range(2)]
        W2 = [arb(4096 + i * 2048, 2048) for i in range(2)]
        RL = [arb(8192 + i * 256, 256) for i in range(2)]
        AA = [arb(8704 + i * 1024, 1024) for i in range(2)]
        bW1 = [Buf(), Buf()]; bW2 = [Buf(), Buf()]; bRL = [Buf(), Buf()]; bAA = [Buf(), Buf()]
        ri = [0]; pi = [0]

        def load_w(fg):
            w1 = W1[fg % 2]; w2 = W2[fg % 2]
            ld("pool", w1.rearrange("p (k c) -> p k c", c=512),
               DAP(ff_w1, l * D * 4 * D + fg * 512, [[4 * D, 128], [128 * 4 * D, 8], [1, 512]]), [], [bW1[fg % 2]])
            ld("pool", w2.rearrange("p (f c) -> p f c", c=D),
               DAP(ff_w2, l * 4 * D * D + fg * 512 * D, [[D, 128], [128 * D, 4], [1, D]]), [], [bW2[fg % 2]])

        def phase1(step):
            fg, b = divmod(step, NB)
            w1 = W1[fg % 2]
            tb = slice(b * TB, (b + 1) * TB)
            aa = AA[step % 2]; ba = bAA[step % 2]
            for f in range(4):
                pb_ = pi[0] % 3; pi[0] += 1
                for k in range(8):
                    mm(PS[pb_][:, :], w1[:, k * 512 + f * 128:k * 512 + (f + 1) * 128], H[:, k, tb], k == 0, k == 7,
                       [bW1[fg % 2], bH[b]], [bPS[pb_]])
                rl = RL[ri[0] % 2]; br = bRL[ri[0] % 2]; ri[0] += 1
                act(rl, PS[pb_][:, :], AF.Relu, [bPS[pb_]], [br])
                tt("pool", aa[:, f * 512:(f + 1) * 512], rl, rl, ALU.mult, [br], [ba])

        def phase2(step):
            fg, b = divmod(step, NB)
            w2 = W2[fg % 2]
            s_ = 0 if b == 0 else 1
            tb = slice(b * TB, (b + 1) * TB)
            aa = AA[step % 2]; ba = bAA[step % 2]
            for m in range(8):
                pb_ = 3 + (m % 4)
                for f in range(4):
                    mm(PS[pb_][:, :], w2[:, f * D + m * 128:f * D + (m + 1) * 128], aa[:, f * 512:(f + 1) * 512], f == 0, f == 3,
                       [bW2[fg % 2], ba], [bPS[pb_]])
                stt("dve", XR[:, m, tb], PS[pb_][:, :], GATE[:, l, 1, m, s_:s_ + 1], XR[:, m, tb], ALU.mult, ALU.add,
                    [bPS[pb_], bMODs[l], bXR[b]], [bXR[b]])
        nsteps = 8 * NB
        load_w(0)
        phase1(0)
        for step in range(nsteps):
            if step + 1 < nsteps:
                if (step + 1) % NB == 0:
                    load_w((step + 1) // NB)
                phase1(step + 1)
            phase2(step)

    LQ = SMALL[0:1, 0:4 * 64]  if False else None
    LQK = nc.alloc_sbuf_tensor("LQK", [1, 256], F32)
    NLAM = nc.alloc_sbuf_tensor("NLAM", [128, 2], F32)
    SGs = nc.alloc_sbuf_tensor("SGs", [128, 2], F32)
    ones1 = nc.alloc_sbuf_tensor("ones1", [1, 128], F32)
    bL = Buf("lam")
    ld("sp", LQK[:], DAP(lam_qk, 0, [[0, 1], [1, 256]]), [], [bL])
    ms("dve", ones1[:], 1.0, [bL])
    ms("dve", SMALL[0:1, 0:8], 0.0, [bSMALL])
    tt("dve", LQK[0:1, 0:64], LQK[0:1, 0:64], LQK[0:1, 64:128], ALU.mult, [bL], [bL])
    tt("dve", LQK[0:1, 128:192], LQK[0:1, 128:192], LQK[0:1, 192:256], ALU.mult, [bL], [bL])
    P.op("dve", lambda e: e.reduce_sum(out=SMALL[0:1, 0:1], in_=LQK[0:1, 0:64], axis=mybir.AxisListType.X), reads=[bL], writes=[bSMALL])
    P.op("dve", lambda e: e.reduce_sum(out=SMALL[0:1, 1:2], in_=LQK[0:1, 128:192], axis=mybir.AxisListType.X), reads=[bL], writes=[bSMALL])
    act(SMALL[0:1, 0:2], SMALL[0:1, 0:2], AF.Exp, [bSMALL], [bSMALL])
    tt("dve", SMALL[0:1, 2:3], SMALL[0:1, 1:2], SMALL[0:1, 0:1], ALU.subtract, [bSMALL], [bSMALL])
    ts("dve", SMALL[0:1, 2:3], SMALL[0:1, 2:3], -LAM_INIT0, None, ALU.add, None, [bSMALL], [bSMALL])
    mm(PS[6][:, 0:1], ones1[0:1, :], SMALL[0:1, 2:3], True, True, [bL, bSMALL], [bPS[6]])
    cp("dve", NLAM[:, 0:1], PS[6][:, 0:1], [bPS[6]], [bL])
    ld("sp", SGs[:, 0:1], DAP(subln, 0, [[1, 128], [1, 1]]), [], [bL])
    ts("dve", SGs[:, 1:2], SGs[:, 0:1], 1.0 - LAM_INIT0, None, ALU.mult, None, [bL], [bL])

    if ph(3):
        norm_to_H(0, 0)

    def inproj():
        WI = arb(0, 8192)
        COS = arf(8192, 2048); SIN = arf(10240, 2048)
        POSR = arf(12288, 2048)
        PERM = arf(14336, 128)
        FI = arf(14464, 2); INV = arf(14466, 2)
        bWI = bWI_pre; bROPE = Buf(); bTMP = Buf()
        ld("sp", FI, c_fi.ap(), [], [bROPE])
        act(INV, FI, AF.Exp, [bROPE], [bROPE], scale=-math.log(10000.0) / 16.0)
        ld("sp", POSR, DAP(c_posr, 0, [[0, 128], [1, 2048]]), [], [bTMP])
        ts("dve", COS, POSR, INV[:, 0:1], None, ALU.mult, None, [bTMP, bROPE], [bROPE])
        ld("sp", POSR, DAP(c_posc, 0, [[0, 128], [1, 2048]]), [bTMP], [bTMP])
        stt("dve", COS, POSR, INV[:, 1:2], COS, ALU.mult, ALU.add, [bTMP, bROPE], [bROPE])
        ts("dve", SIN, COS, 1.0 / (2 * math.pi), 12582912.0, ALU.mult, ALU.add, [bROPE], [bROPE])
        ts("dve", SIN, SIN, -12582912.0, None, ALU.add, None, [bROPE], [bROPE])
        stt("dve", SIN, COS, 1.0 / (2 * math.pi), SIN, ALU.mult, ALU.subtract, [bROPE], [bROPE])
        ts("dve", COS, SIN, 0.25, 12582912.0, ALU.add, ALU.add, [bROPE], [bROPE])
        ts("dve", COS, COS, -12582912.0, None, ALU.add, None, [bROPE], [bROPE])
        stt("dve", COS, SIN, 0.25, COS, ALU.add, ALU.subtract, [bROPE], [bROPE])
        act(COS, COS, AF.Sin, [bROPE], [bROPE], scale=2 * math.pi)
        act(SIN, SIN, AF.Sin, [bROPE], [bROPE], scale=2 * math.pi)
        ms("dve", PERM, 0.0, [bROPE])
        for base in range(0, 128, 32):
            ts("dve", PERM[:, base:base + 16], ident[:, base + 16:base + 32], -1.0, None, ALU.mult, None, [bC], [bROPE])
            cp("dve", PERM[:, base + 16:base + 32], ident[:, base:base + 16], [bC], [bROPE])
        QF = [arf(14592 + i * 512, 512) for i in range(2)]
        T1 = arf(15616, 512)
        STG = [arb(16128 + i * 256, 256) for i in range(4)]
        VF = [arf(12288 + i * 512, 512) for i in range(2)]
        VB = [arb(13312 + i * 256, 256) for i in range(2)]
        bQF = [Buf(), Buf()]; bT1 = Buf(); bSTG = [Buf() for _ in range(4)]; bVF = [Buf(), Buf()]; bVB = [Buf(), Buf()]
        si = 0; qi = 0; vi = 0; pi = 0
        if SUB < 1:
            return
        for b in range(NB):
            if b >= SUBB:
                break
            tb = slice(b * TB, (b + 1) * TB)
            for ot in list(range(0, 8)) + list(range(12, 16)):
                pb_ = pi % 2; pi += 1
                for k in range(8):
                    mm(PS[pb_][:, :], WI[:, k * 2048 + ot * 128:k * 2048 + (ot + 1) * 128], H[:, k, tb], k == 0, k == 7,
                       [bWI, bH[b]], [bPS[pb_]])
                stg = STG[si % 4]; bs = bSTG[si % 4]; si += 1
                if ot < 8 and b >= 1:
                    lt = slice((b - 1) * TB, b * TB)
                    qf = QF[qi % 2]; bq = bQF[qi % 2]; qi += 1
                    act(qf, PS[pb_][:, :], AF.Copy, [bPS[pb_]], [bq])
                    mm(PS[2][:, :], PERM, qf, True, True, [bROPE, bq], [bPS[2]])
                    tt("dve", T1, qf, COS[:, lt], ALU.mult, [bq, bROPE], [bT1])
                    tt("dve", qf, PS[2][:, :], SIN[:, lt], ALU.mult, [bPS[2], bROPE, bq], [bq])
                    tt("pool", stg, T1, qf, ALU.add, [bT1, bq], [bs])
                else:
                    act(stg, PS[pb_][:, :], AF.Copy, [bPS[pb_]], [bs])
                if ot < 4:
                    dst, db = DAP(QT, ot * 128 * NT + b * TB, [[NT, 128], [1, TB]]), bQT
                elif ot < 8:
                    dst, db = DAP(KT, (ot - 4) * 128 * 3072 + b * TB, [[3072, 128], [1, TB]]), bKT
                else:
                    dst, db = DAP(UT, (ot - 12) * 128 * NT + b * TB, [[NT, 128], [1, TB]]), bUT
                ld("sp", dst, stg, [bs], [db])
            for t4 in range(4 if SUB >= 3 else 0):
                tok = slice(b * TB + t4 * 128, b * TB + (t4 + 1) * 128)
                for which in ((1, 2) if b == 0 else (2,)):
                    pb_ = 3 + (pi % 2); pi += 1
                    for k in range(8):
                        mm(PS[pb_][:, :], H[:, k, tok], WI[:, k * 2048 + which * 512:k * 2048 + (which + 1) * 512], k == 0, k == 7,
                           [bWI, bH[b]], [bPS[pb_]])
                    if which == 2:
                        vb = VB[vi % 2]; bv = bVB[vi % 2]
                        act(vb, PS[pb_][:, :], AF.Copy, [bPS[pb_]], [bv])
                        if not os.environ.get("TM_NOVS"):
                            ld("sp", DAP(VS, (b * TB + t4 * 128) * 512, [[512, 128], [1, 512]]), vb, [bv], [bVS])
                    if b == 0 and not os.environ.get("TM_NOOUT"):
                        vf = VF[vi % 2]; bf_ = bVF[vi % 2]
                        act(vf, PS[pb_][:, :], AF.Copy, [bPS[pb_]], [bf_])
                        dd = nk if which == 1 else nv
                        if os.environ.get("TM_OUTSCR"):
                            ld("sp", DAP(DBG, (t4 * 128) * 512, [[512, 128], [1, 512]]), vf, [bf_], [])
                        else:
                            out_events.append(ld("sp", DAP(dd, (t4 * 128) * 512, [[512, 128], [1, 512]]), vf, [bf_], []))
                    vi += 1
        for t4 in range(4 if SUB >= 4 else 0):
            vf = VF[vi % 2]; bf_ = bVF[vi % 2]; vi += 1
            ld("sp", vf, DAP(ck, t4 * 128 * 512, [[512, 128], [1, 512]]), [bf_], [bf_])
            for hh in range(4):
                tr(PS[5][:, hh * 128:(hh + 1) * 128], vf[:, hh * 128:(hh + 1) * 128], ident[:], [bf_, bC], [bPS[5]])
            stg = STG[si % 4]; bs = bSTG[si % 4]; si += 1
            act(stg, PS[5][:, :], AF.Copy, [bPS[5]], [bs])
            ld("sp", DAP(KT, 2560 + t4 * 128, [[3072, 128], [128 * 3072, 4], [1, 128]]),
               stg.rearrange("p (h t) -> p h t", t=128), [bs], [bKT])
            vb = VB[vi % 2]; bv = bVB[vi % 2]
            ld("pool", vb, DAP(cv, t4 * 128 * 512, [[512, 128], [1, 512]]), [bv], [bv])
            ld("sp", DAP(VS, (2560 + t4 * 128) * 512, [[512, 128], [1, 512]]), vb, [bv], [bVS])
    P.barrier()
    if ph(4):
        inproj()
    P.barrier()

    def attention():
        KTh = [arb(i * 1536, 1536) for i in range(2)]
        Vh = [arb(3072 + i * 1280, 1280) for i in range(2)]
        Qh = [arb(5632 + i * 1024, 1024) for i in range(2)]
        Qh1 = [arb(12544 + i * 1024, 1024) for i in range(2)]
        PT = [arb(7680 + i * 256, 256) for i in range(4)]
        R1s = [arf(8704, 512), arf(11520, 512)]; R2s = [arf(9216, 512), arf(12032, 512)]
        O1s = [arf(9728, 512), arf(14592, 512)]; O2s = [arf(10240, 512), arf(15104, 512)]
        SQb = arb(10752, 256); AOb = [arb(11008 + i * 256, 256) for i in range(2)]
        bK = [Buf(), Buf()]; bV = [Buf(), Buf()]; bQ = [Buf(), Buf()]; bPT = [Buf() for _ in range(4)]
        bRs = [Buf(), Buf()]; bOs = [Buf(), Buf()]; bSQb = Buf(); bAO = [Buf(), Buf()]
        for i in range(2):
            ms("dve", Qh[i][64:128, :], 0.0, [bQ[i]])
            ms("dve", Qh1[i][0:64, :], 0.0, [bQ[i]])
        heads = []
        for (q0, nq, k0, nkeys) in ((0, 256, 0, 256), (256, 256, 256, 256), (512, 2048, 512, 2560)):
            for h in range(4):
                heads.append((q0, nq, k0, nkeys, h))

        def load_head(hx):
            q0, nq, k0, nkeys, h = heads[hx]
            nkt = nkeys // 128
            p = hx % 2
            ld("sp", KTh[p][:, 0:nkeys], DAP(KT, h * 128 * 3072 + k0, [[3072, 128], [1, nkeys]]), [bKT], [bK[p]])
            ld("sp", Vh[p][:, 0:nkt * 128].rearrange("p (t e) -> p t e", e=128),
               DAP(VS, k0 * 512 + h * 128, [[512, 128], [128 * 512, nkt], [1, 128]]), [bVS], [bV[p]])
            ld("sp", Qh[p][0:64, 0:nq], DAP(QT, h * 128 * NT + q0, [[NT, 64], [1, nq]]), [bQT], [bQ[p]])
            ld("sp", Qh1[p][64:128, 0:nq], DAP(QT, (h * 128 + 64) * NT + q0, [[NT, 64], [1, nq]]), [bQT], [bQ[p]])
        loops = []
        qbi = 0
        for hx, (q0, nq, k0, nkeys, h) in enumerate(heads):
            for qb in range(0, nq, 512):
                for m in range(2):
                    loops.append((hx, qb, min(512, nq - qb), m, nkeys // 128, qbi))
                qbi += 1
        steps = [(li, kt) for li, L in enumerate(loops) for kt in range(L[4])]
        SB_ = (0, 1, 7)
        sbank = {}

        def issue_S(si):
            li, kt = steps[si]
            hx, qb, nqb, m, nkt, _ = loops[li]
            p = hx % 2
            psb = SB_[si % 3]
            sbank[si] = psb
            mm(PS[psb][:, 0:nqb], KTh[p][:, kt * 128:(kt + 1) * 128], (Qh[p] if m == 0 else Qh1[p])[:, qb:qb + nqb],
               True, True, [bK[p], bQ[p]], [bPS[psb]])

        def partA(L):
            hx, qb, nqb, m, nkt, qi = L
            e = qi % 2
            R = (R1s if m == 0 else R2s)[e]; O = (O1s if m == 0 else O2s)[e]
            P.op("dve", lambda e_, n=nqb, R=R, m=m: e_.reciprocal(out=R[:, 0:n], in_=PS[4 + m][:, 0:n]), reads=[bPS[4 + m]], writes=[bRs[e]])
            tt("dve", O[:, 0:nqb], PS[2 + m][:, 0:nqb], R[:, 0:nqb], ALU.mult, [bPS[2 + m], bRs[e]], [bOs[e]])

        aoi = [0]

        def partB(L):
            hx, qb, nqb, m, nkt, qi = L
            q0, nq, k0, nkeys, h = heads[hx]
            e = qi % 2
            O1 = O1s[e]; O2 = O2s[e]; R1 = R1s[e]
            stt("dve", O1[:, 0:nqb], O2[:, 0:nqb], NLAM[:, 0:1], O1[:, 0:nqb], ALU.mult, ALU.add, [bOs[e], bL], [bOs[e]])
            tt("dve", SQb[:, 0:nqb], O1[:, 0:nqb], O1[:, 0:nqb], ALU.mult, [bOs[e]], [bSQb])
            mm(PS[6][:, 0:nqb], onesb[:], SQb[:, 0:nqb], True, True, [bC, bSQb], [bPS[6]])
            act(R1[:, 0:nqb], PS[6][:, 0:nqb], AF.Ln, [bPS[6], bRs[e]], [bRs[e]], scale=1.0 / 128, bias=EPS)
            act(R1[:, 0:nqb], R1[:, 0:nqb], AF.Exp, [bRs[e]], [bRs[e]], scale=-0.5)
            ao = AOb[aoi[0] % 2]; ba = bAO[aoi[0] % 2]; aoi[0] += 1
            stt("dve", ao[:, 0:nqb], O1[:, 0:nqb], SGs[:, 1:2], R1[:, 0:nqb], ALU.mult, ALU.mult, [bOs[e], bRs[e], bL], [ba])
            ld("sp", DAP(MO, h * 128 * NT + q0 + qb, [[NT, 128], [1, nqb]]), ao[:, 0:nqb], [ba], [bMO])
        MW1 = [arb(15616 + i * 2048, 2048) for i in range(2)]; bMW1 = [Buf(), Buf()]
        mod_pending = list(range(12))
        load_head(0)
        issue_S(0)
        if len(steps) > 1:
            issue_S(1)
        deferred = None
        pti = 0
        last_hx = -1
        for si, (li, kt) in enumerate(steps):
            L = loops[li]
            hx, qb, nqb, m, nkt, qi = L
            p = hx % 2
            if hx != last_hx:
                last_hx = hx
                if hx + 1 < len(heads):
                    load_head(hx + 1)
            psb = sbank.pop(si)
            pt = PT[pti % 4]; bp = bPT[pti % 4]; pti += 1
            act(pt[:, 0:nqb], PS[psb][:, 0:nqb], AF.Exp, [bPS[psb]], [bp], scale=0.125)
            if si + 2 < len(steps):
                issue_S(si + 2)
            mm(PS[2 + m][:, 0:nqb], Vh[p][:, kt * 128:(kt + 1) * 128], pt[:, 0:nqb], kt == 0, kt == nkt - 1, [bV[p], bp], [bPS[2 + m]])
            mm(PS[4 + m][:, 0:nqb], onesb[:], pt[:, 0:nqb], kt == 0, kt == nkt - 1, [bC, bp], [bPS[4 + m]])
            if deferred is not None and kt == min(3, nkt - 1):
                partB(deferred)
                deferred = None
            if mod_pending and si % 48 == 40:
                mod_chunk(1, mod_pending.pop(0), MW1, bMW1, bank=6, per_chunk=True)
            if kt == nkt - 1:
                partA(L)
                if m == 1:
                    deferred = L
        if deferred is not None:
            partB(deferred)
        while mod_pending:
            mod_chunk(1, mod_pending.pop(0), MW1, bMW1, bank=6, per_chunk=True)
        mod_finish(1, evac=False)
    if ph(6):
        attention()

    def s5():
        HF = H[:, :, :].rearrange("p k t -> p (k t)").bitcast(F32)
        o = [0]; oh = [0]

        def af(n):
            r = arf(o[0], n); o[0] += n; assert o[0] <= ARN; return r

        def ab(n):
            r = arb(o[0], n); o[0] += n; assert o[0] <= ARN; return r

        def hf(n):
            r = HF[:, oh[0]:oh[0] + n]; oh[0] += n; assert oh[0] <= 10240; return r

        def hb(n):
            r = HF[:, oh[0]:oh[0] + n].bitcast(BF16); oh[0] += n; assert oh[0] <= 10240; return r
        PT_ = af(96)
        XRI = af(64)
        PRE = af((T + 1) * 32); PIM = af((T + 1) * 32)
        FF = af(6 * 32)
        BMR = af(1024); BMI = af(1024)
        CMR = af(1024); CMI = af(1024)
        DV = af(4); DIAG = af(128)
        LAMR = hf(3 * 128)
        TA = hf((T + 1) * 32); TBb = hf((T + 1) * 32)
        oh_bre = oh[0]
        BRE = hf(512); BIM = hf(512); BBR = hf(512); BBI = hf(512)
        CN = hf(512)
        bPrep = Buf("s5prep")
        CN2 = [af(128), af(128)]; bCN2 = [Buf(), Buf()]
        ld("sp", LAMR[0:32, 0:128], DAP(lam_re, 0, [[128, 32], [1, 128]]), [], [bPrep])
        ld("sp", LAMR[0:32, 128:256], DAP(lam_im, 0, [[128, 32], [1, 128]]), [], [bPrep])
        ld("sp", FF[0:32, 0:2], DAP(log_dt, 0, [[2, 32], [1, 2]]), [], [bPrep])
        cp("dve", LAMR[0:32, 256:384].rearrange("p (g q) -> p g q", q=64), FF[0:32, 0:2].unsqueeze(2).to_broadcast([32, 2, 64]), [bPrep], [bPrep])
        for i in range(3):
            tr(PS[0][:, i * 32:(i + 1) * 32], LAMR[0:32, i * 128:(i + 1) * 128], ident[0:32, 0:32], [bPrep, bC], [bPS[0]])
        cp("dve", PT_, PS[0][:, 0:96], [bPS[0]], [bPrep])
        LR = PT_[:, 0:32]; LI = PT_[:, 32:64]; DT = PT_[:, 64:96]
        act(DT, DT, AF.Exp, [bPrep], [bPrep])
        tt("dve", XRI[:, 0:32], LR, DT, ALU.mult, [bPrep], [bPrep])
        tt("dve", XRI[:, 32:64], LI, DT, ALU.mult, [bPrep], [bPrep])
        for tau in range(T + 1):
            ts("dve", PRE[:, tau * 32:(tau + 1) * 32], XRI[:, 0:32], float(tau), None, ALU.mult, None, [bPrep], [bPrep])
            ts("dve", TA[:, tau * 32:(tau + 1) * 32], XRI[:, 32:64], float(tau), None, ALU.mult, None, [bPrep], [bPrep])
        act(PRE, PRE, AF.Exp, [bPrep], [bPrep])
        ts("dve", TBb, TA, 1.0 / (2 * math.pi), 12582912.0, ALU.mult, ALU.add, [bPrep], [bPrep])
        ts("dve", TBb, TBb, -12582912.0, None, ALU.add, None, [bPrep], [bPrep])
        stt("dve", PIM, TA, 1.0 / (2 * math.pi), TBb, ALU.mult, ALU.subtract, [bPrep], [bPrep])
        ts("dve", TA, PIM, 0.25, 12582912.0, ALU.add, ALU.add, [bPrep], [bPrep])
        ts("dve", TA, TA, -12582912.0, None, ALU.add, None, [bPrep], [bPrep])
        stt("dve", TA, PIM, 0.25, TA, ALU.add, ALU.subtract, [bPrep], [bPrep])
        act(TA, TA, AF.Sin, [bPrep], [bPrep], scale=2 * math.pi)
        act(PIM, PIM, AF.Sin, [bPrep], [bPrep], scale=2 * math.pi)
        tt("dve", PIM, PIM, PRE, ALU.mult, [bPrep], [bPrep])
        tt("dve", PRE, TA, PRE, ALU.mult, [bPrep], [bPrep])
        A1R = PRE[:, 32:64]; A1I = PIM[:, 32:64]
        F0, F1, F2, F3, F4, F5 = [FF[:, i * 32:(i + 1) * 32] for i in range(6)]
        tt("dve", F0, LR, LR, ALU.mult, [bPrep], [bPrep])
        tt("dve", F1, LI, LI, ALU.mult, [bPrep], [bPrep])
        tt("dve", F0, F0, F1, ALU.add, [bPrep], [bPrep])
        P.op("dve", lambda e: e.reciprocal(out=F0, in_=F0), reads=[bPrep], writes=[bPrep])
        ts("dve", F1, A1R, -1.0, None, ALU.add, None, [bPrep], [bPrep])
        tt("dve", F2, F1, LR, ALU.mult, [bPrep], [bPrep])
        tt("dve", F3, A1I, LI, ALU.mult, [bPrep], [bPrep])
        tt("dve", F2, F2, F3, ALU.add, [bPrep], [bPrep])
        tt("dve", F2, F2, F0, ALU.mult, [bPrep], [bPrep])
        tt("dve", F3, A1I, LR, ALU.mult, [bPrep], [bPrep])
        tt("dve", F4, F1, LI, ALU.mult, [bPrep], [bPrep])
        tt("dve", F3, F3, F4, ALU.subtract, [bPrep], [bPrep])
        tt("dve", F3, F3, F0, ALU.mult, [bPrep], [bPrep])
        for d_ in range(2):
            ld("sp", BRE.rearrange("p (a d h) -> p a d h", d=2, h=16)[:, :, d_, :], DAP(b_re, d_ * 32768, [[16, 128], [2048, 16], [1, 16]]), [], [bPrep])
            ld("sp", BIM.rearrange("p (a d h) -> p a d h", d=2, h=16)[:, :, d_, :], DAP(b_im, d_ * 32768, [[16, 128], [2048, 16], [1, 16]]), [], [bPrep])
        fo = lambda Fx: bass.AP(Fx.tensor, Fx.offset, [list(Fx.ap[0]), [1, 16], [16, 2], [0, 16]])
        V4 = lambda x: x.rearrange("p (a d h) -> p a d h", d=2, h=16)
        tt("dve", V4(BBR), V4(BRE), fo(F2), ALU.mult, [bPrep], [bPrep])
        tt("dve", V4(BBI), V4(BIM), fo(F3), ALU.mult, [bPrep], [bPrep])
        tt("dve", BBR, BBR, BBI, ALU.subtract, [bPrep], [bPrep])
        tt("dve", V4(BBI), V4(BRE), fo(F3), ALU.mult, [bPrep], [bPrep])
        tt("dve", V4(BRE), V4(BIM), fo(F2), ALU.mult, [bPrep], [bPrep])
        tt("dve", BBI, BBI, BRE, ALU.add, [bPrep], [bPrep])
        ms("pool", BMR, 0.0, [bPrep]); ms("pool", BMI, 0.0, [bPrep])

        def M5(x, p0, p1, g2):
            v = x[p0:p1, :].rearrange("p (a d g h) -> p a d g h", d=2, g=2, h=16)
            return v[:, :, :, g2, :]
        for (src, dst) in ((BBR, BMR), (BBI, BMI)):
            cp("dve", M5(dst, 0, 64, 0), V4(src)[0:64], [bPrep], [bPrep])
            cp("dve", M5(dst, 64, 128, 1), V4(src)[64:128], [bPrep], [bPrep])
        ms("pool", CMR, 0.0, [bPrep]); ms("pool", CMI, 0.0, [bPrep])
        for (csrc, cdst) in ((c_re, CMR), (c_im, CMI)):
            ld("sp", CN.rearrange("p (a q) -> p a q", q=64), DAP(csrc, 0, [[64, 128], [8192, 8], [1, 64]]), [bPrep], [bPrep])
            for dk in range(8):
                pb_ = dk % 2
                cn2 = CN2[dk % 2]
                act(cn2.rearrange("p (g q) -> p g q", q=64), bass.AP(CN.tensor, CN.offset + dk * 64, [list(CN.ap[0]), [0, 2], [1, 64]]), AF.Copy, [bPrep], [bCN2[dk % 2]])
                tr(PS[pb_][:, 0:128], cn2, ident[:], [bCN2[dk % 2], bC], [bPS[pb_]])
                pv = PS[pb_][:, 0:128].rearrange("p (q g h) -> p q g h", g=2, h=16)
                cd = cdst[:, dk * 128:(dk + 1) * 128].rearrange("p (q g h) -> p q g h", g=2, h=16)
                act(cd[0:64, :, 0, :], pv[0:64, :, 0, :], AF.Copy, [bPS[pb_]], [bPrep])
                act(cd[64:128, :, 1, :], pv[64:128, :, 1, :], AF.Copy, [bPS[pb_]], [bPrep])
        ld("sp", DV, DAP(s5_d, 0, [[1, 128], [128, 4]]), [], [bPrep], **SL)

        NSL = NCL + 1; NSC = NCC // 2 + 1
        ZL = hb(32 * NSL)
        ZC = hb(32 * 2 * NSC)
        ZLv = ZL.rearrange("p (r c s) -> p r c s", r=2, c=32)
        ZCv = ZC.rearrange("p (r c b s) -> p r c b s", r=2, c=32, b=2)
        CA = hf(64); CB = hf(64); STI = hf(64)
        RSL = [hf(64) for _ in range(2)]
        RSC = [hf(128) for _ in range(2)]
        S1 = hf(128); S2 = hf(128); FIN = hf(128)
        bZ = Buf("zs")
        pw = lambda X: bass.AP(X.tensor, X.offset + T * 32, [list(X.ap[0]), [1, 16], [16, 2]])
        c3 = lambda X, h: X[:, h * 32:(h + 1) * 32].rearrange("p (a d) -> p a d", d=2)
        cp("dve", c3(CA, 0), pw(PRE), [bPrep], [bZ]); cp("dve", c3(CA, 1), pw(PRE), [bPrep], [bZ])
        ts("dve", c3(CB, 0), pw(PIM), -1.0, None, ALU.mult, None, [bPrep], [bZ]); cp("dve", c3(CB, 1), pw(PIM), [bPrep], [bZ])
        for d_ in range(2):
            ld("sp", STI.rearrange("p (a d r) -> p a d r", d=2, r=2)[:, :, d_, :], DAP(st, d_ * 4096, [[2, 128], [256, 16], [1, 2]]), [], [bZ])
        for r in range(2):
            cp("dve", RSL[0][:, r * 32:(r + 1) * 32].rearrange("p (a d) -> p a d", d=2),
               STI.rearrange("p (a d r) -> p a d r", d=2, r=2)[:, :, :, r], [bZ], [bZ])
        cp("dve", ZLv[:, :, :, 0], RSL[0].rearrange("p (r c) -> p r c", r=2), [bZ], [bZ])
        ms("pool", RSC[0], 0.0, [bZ])
        ms("pool", ZCv[:, :, :, :, 0], 0.0, [bZ])

        TABS = [ab(4096), ab(4096)]
        TZS = [ab(2048), HF[:, oh_bre:oh_bre + 2048].bitcast(BF16)]
        GEN = [af(512) for _ in range(2)]
        o_g = o[0]
        G1 = af(256); G2 = af(256)
        UK = ab(1280)
        bTABS = [Buf(), Buf()]; bTZS = [Buf(), Buf()]; bGEN = [Buf(), Buf()]; bG = Buf(); bUK = Buf()
        bTZS[1].w = bPrep.w; bTZS[1].r = list(bPrep.r)
        tabvs = [t_.rearrange("p (d t r c) -> p d t r c", d=2, t=T, r=2) for t_ in TABS]
        tzvs = [t_.rearrange("p (t c) -> p t c", c=128) for t_ in TZS]

        def powv(X, tau, k):
            return bass.AP(X.tensor, X.offset + tau * 32 + k * 4, [list(X.ap[0]), [16, 2], [1, 4], [0, 32]])

        def bmv(X, k):
            return bass.AP(X.tensor, X.offset + k * 4 * 64, [list(X.ap[0]), [32, 2], [64, 4], [1, 32]])

        def cmv(X, k):
            return bass.AP(X.tensor, X.offset + k * 128, [list(X.ap[0]), [512, 2], [32, 4], [1, 32]])
        g4 = lambda X: X.rearrange("p (d q c) -> p d q c", d=2, q=4)
        gi = [0]

        def cgen(XR_, XI_, tau, k, negim):
            gt_ = GEN[gi[0] % 2]; bg = bGEN[gi[0] % 2]; gi[0] += 1
            re = g4(gt_[:, 0:256]); im = g4(gt_[:, 256:512])
            tt("dve", re, XR_, powv(PRE, tau, k), ALU.mult, [bPrep], [bg])
            tt("dve", g4(G2), XI_, powv(PIM, tau, k), ALU.mult, [bPrep], [bG])
            tt("dve", re, re, g4(G2), ALU.subtract, [bG], [bg])
            tt("dve", im, XR_, powv(PIM, tau, k), ALU.mult, [bPrep], [bg])
            tt("dve", g4(G1), XI_, powv(PRE, tau, k), ALU.mult, [bPrep], [bG])
            if negim:
                stt("dve", im, im, -1.0, g4(G1), ALU.mult, ALU.subtract, [bG], [bg])
            else:
                tt("dve", im, im, g4(G1), ALU.add, [bG], [bg])
            return gt_, bg

        zi = [0]

        def gen1(k, tau):
            tabv = tabvs[k % 2]; bTAB = bTABS[k % 2]
            gt_, bg = cgen(bmv(BMR, k), bmv(BMI, k), tau, k, False)
            for r in range(2):
                for d in range(2):
                    idx = r * 2 + d
                    tr(PS[idx % 2][:, (idx // 2) * 128:(idx // 2 + 1) * 128],
                       gt_[:, r * 256 + d * 128:r * 256 + (d + 1) * 128], ident[:], [bg, bC], [bPS[idx % 2]])
            for r in range(2):
                for d in range(2):
                    idx = r * 2 + d
                    src = PS[idx % 2][:, (idx // 2) * 128:(idx // 2 + 1) * 128]
                    act(tabv[:, d, tau, r, :], src, AF.Copy, [bPS[idx % 2]], [bTAB])

        def zmm4(k, d, r):
            tabv = tabvs[k % 2]; bTAB = bTABS[k % 2]
            for (u0, n, c0) in ((512, NCL, 0), (0, NCC, 128)):
                for i in range(T):
                    tau = (T - 1 - i) if d == 0 else i
                    for q in range(4):
                        lhs = tabv[32 * q:32 * q + 32, d, tau, r, :]
                        mm(PS[2 + q][:, c0:c0 + n], lhs, AP(UK, u0 + i, [[T, n]], 32 * q, 32 * q + 32), i == 0, i == T - 1,
                           [bTAB, bUK], [bPS[2 + q]], tile_position=(32 * q, 0))

        def zev4(k, d, r):
            for q in range(4):
                col = k * 8 + q * 2 + d
                pb_ = 2 + q
                pc = PS[pb_][:, 128:128 + NCC].rearrange("p (b s) -> p b s", b=2)
                if d == 0:
                    act(ZLv[:, r, col, 1:NSL], PS[pb_][:, 0:NCL], AF.Copy, [bPS[pb_]], [bZ])
                    act(ZCv[:, r, col, :, 1:NSC], pc, AF.Copy, [bPS[pb_]], [bZ])
                else:
                    act(ZLv[:, r, col, NSL - 1:0:-1], PS[pb_][:, 0:NCL], AF.Copy, [bPS[pb_]], [bZ])
                    act(ZCv[:, r, col, :, NSC - 1:0:-1], pc, AF.Copy, [bPS[pb_]], [bZ])
        for tau in range(T):
            gen1(0, tau)
        for k in range(4):
            ld("sp", UK, DAP(UT, k * 128 * NT, [[NT, 128], [1, NT]]), [bUT, bUK], [bUK])
            for g_ in range(4):
                d, r = g_ // 2, g_ % 2
                zmm4(k, d, r)
                if k + 1 < 4:
                    for t_ in range(4):
                        gen1(k + 1, g_ * 4 + t_)
                zev4(k, d, r)
        r3 = lambda X: X.rearrange("p (r c) -> p r c", r=2)
        bS = Buf("scanscratch"); bS2 = Buf("scanscratch2")
        bRSL = [Buf(), Buf()]; bRSC = [Buf(), Buf()]
        bZHL = [Buf(), Buf()]; bZHC = [Buf(), Buf()]
        for bb in bRSL + bRSC + bZHL + bZHC:
            bb.w = bZ.w; bb.r = list(bZ.r)
        for s in range(NCL):
            prev = RSL[s % 2]; nxt = RSL[(s + 1) % 2]
            bp_ = bRSL[s % 2]; bn_ = bRSL[(s + 1) % 2]; bh_ = bZHL[(s + 1) % 2]
            sw = bass.AP(prev.tensor, prev.offset + 32, [list(prev.ap[0]), [-32, 2], [1, 32]])
            tt("dve", r3(S1[:, 0:64]), r3(prev), r3(CA), ALU.mult, [bp_], [bS])
            tt("pool", r3(S2[:, 0:64]), sw, r3(CB), ALU.mult, [bp_], [bS2])
            tt("dve", r3(nxt), ZLv[:, :, :, s + 1], r3(S1[:, 0:64]), ALU.add, [bh_, bS], [bn_])
            tt("dve", r3(nxt), r3(nxt), r3(S2[:, 0:64]), ALU.add, [bS2], [bn_])
            act(ZLv[:, :, :, s + 1], r3(nxt), AF.Copy, [bn_], [bh_])
        r4 = lambda X: X.rearrange("p (r c b) -> p r c b", r=2, c=32)
        c4 = lambda X: X.rearrange("p (r c) -> p r c", r=2).unsqueeze(3).to_broadcast([128, 2, 32, 2])
        for s in range(NSC - 1):
            prev = RSC[s % 2]; nxt = RSC[(s + 1) % 2]
            bp_ = bRSC[s % 2]; bn_ = bRSC[(s + 1) % 2]; bh_ = bZHC[(s + 1) % 2]
            sw = bass.AP(prev.tensor, prev.offset + 64, [list(prev.ap[0]), [-64, 2], [2, 32], [1, 2]])
            tt("dve", r4(S1), r4(prev), c4(CA), ALU.mult, [bp_, bS], [bS])
            tt("pool", r4(S2), sw, c4(CB), ALU.mult, [bp_], [bS2])
            tt("dve", r4(nxt), ZCv[:, :, :, :, s + 1], r4(S1), ALU.add, [bh_, bS], [bn_])
            tt("dve", r4(nxt), r4(nxt), r4(S2), ALU.add, [bS2], [bn_])
            act(ZCv[:, :, :, :, s + 1], r4(nxt), AF.Copy, [bn_], [bh_])
        fin = RSC[(NSC - 1) % 2]
        fv = FIN.rearrange("p (b d a r) -> p b d a r", b=2, d=2, a=16)
        alldeps = bRSL + bRSC + bZHL + bZHC + [bZ]
        for r in range(2):
            for b_ in range(2):
                src = bass.AP(fin.tensor, fin.offset + r * 64 + b_, [list(fin.ap[0]), [2, 2], [4, 16]])
                cp("dve", fv[:, b_, :, :, r], src, alldeps, [bZ])
        out_events.append(ld("sp", DAP(ns, 0, [[2, 128], [256, 64], [1, 2]]), FIN.rearrange("p (x r) -> p x r", r=2), [bZ], []))

        _ys = af(512); YS = [_ys, _ys]; _by = Buf(); bYS = [_by, _by]
        BMK = [[af(128), af(128)], [af(128), af(128)]]; bBMK = Buf()
        _gs = ab(256); GS_ = [_gs, _gs]; _bg = Buf(); bGS_ = [_bg, _bg]
        Y1 = arf(o_g, 512); bY1 = bG
        yi = [0]

        def prep2(k):
            ts("dve", DIAG, ident[:], DV[:, k:k + 1], None, ALU.mult, None, [bC, bPrep], [bPrep])
            for d in range(2):
                cp("dve", BMK[0][d].rearrange("p (q c) -> p q c", c=32), bmv(BMR, k)[:, d, :, :], [bPrep], [bBMK])
                cp("dve", BMK[1][d].rearrange("p (q c) -> p q c", c=32), bmv(BMI, k)[:, d, :, :], [bPrep], [bBMK])

        def gen2(k, tau):
            tabv = tabvs[k % 2]; bTAB = bTABS[k % 2]; tzv = tzvs[k % 2]; bTZ = bTZS[k % 2]
            gt_, bg = cgen(cmv(CMR, k), cmv(CMI, k), tau, k, True)
            if tau >= 1:
                for r in range(2):
                    act(tabv[:, 0, tau - 1, r, :], gt_[:, r * 256:r * 256 + 128], AF.Copy, [bg], [bTAB])
                    act(tabv[:, 1, T - tau, r, :], gt_[:, r * 256 + 128:r * 256 + 256], AF.Copy, [bg], [bTAB])
            if tau <= T - 1:
                for d in range(2):
                    mm(PS[d][:, 0:128], BMK[0][d], gt_[:, d * 128:(d + 1) * 128], True, False, [bBMK, bg], [bPS[d]])
                    mm(PS[d][:, 0:128], BMK[1][d], gt_[:, 256 + d * 128:256 + (d + 1) * 128], False, True, [bBMK, bg], [bPS[d]])
                if tau == 0:
                    tt("dve", G1[:, 0:128], PS[0][:, 0:128], maskq[:], ALU.mult, [bPS[0], bC], [bG])
                    tt("dve", G2[:, 0:128], PS[1][:, 0:128], maskq[:], ALU.mult, [bPS[1], bC], [bG])
                    tt("dve", G1[:, 0:128], G1[:, 0:128], G2[:, 0:128], ALU.add, [bG], [bG])
                    tt("dve", tzv[:, T - 1, :], G1[:, 0:128], DIAG, ALU.add, [bG, bPrep], [bTZ])
                else:
                    tt("dve", tzv[:, T - 1 + tau, :], PS[0][:, 0:128], maskq[:], ALU.mult, [bPS[0], bC], [bTZ])
                    tt("dve", tzv[:, T - 1 - tau, :], PS[1][:, 0:128], maskq[:], ALU.mult, [bPS[1], bC], [bTZ])

        def outj(k, j):
            tabv = tabvs[k % 2]; bTAB = bTABS[k % 2]; tzv = tzvs[k % 2]; bTZ = bTZS[k % 2]
            outl = PS[2 + j // 4][:, (j % 4) * 128:(j % 4 + 1) * 128]; bol = bPS[2 + j // 4]
            outc = PS[6][:, j * 32:(j + 1) * 32]; boc = bPS[6]
            for i in range(T):
                mm(outl, tzv[:, T - 1 + (j - i), :], AP(UK, 512 + i, [[T, NCL]]), i == 0, False, [bTZ, bUK], [bol])
            for i in range(T):
                mm(outc, tzv[:, T - 1 + (j - i), :], AP(UK, i, [[T, NCC]]), i == 0, False, [bTZ, bUK], [boc])
            nn = NSC - 1
            for d in range(2):
                for r in range(2):
                    for q in range(4):
                        col = k * 8 + q * 2 + d
                        lhs = tabv[:, d, j, r, 32 * q:32 * q + 32]
                        base = (r * 32 + col) * NSL
                        rhs = AP(ZL, base, [[1, NCL]]) if d == 0 else AP(ZL, base + NCL - 1, [[-1, NCL]])
                        mm(outl[32 * q:32 * q + 32], lhs, rhs, False, (d == 1 and r == 1), [bTAB, bZ], [bol], tile_position=(0, 32 * q))
            for d in range(2):
                for r in range(2):
                    for b_ in range(2):
                        for q in range(4):
                            col = k * 8 + q * 2 + d
                            lhs = tabv[:, d, j, r, 32 * q:32 * q + 32]
                            base = (r * 32 + col) * 2 * NSC + b_ * NSC
                            rhs = AP(ZC, base, [[1, nn]]) if d == 0 else AP(ZC, base + nn - 1, [[-1, nn]])
                            mm(outc[32 * q:32 * q + 32, b_ * nn:(b_ + 1) * nn], lhs, rhs, False, (d == 1 and r == 1 and b_ == 1), [bTAB, bZ], [boc], tile_position=(0, 32 * q))

        def evac(k):
            for blk in range(5):
                ys = YS[yi[0] % 2]; by = bYS[yi[0] % 2]; gs_ = GS_[yi[0] % 2]; bgs = bGS_[yi[0] % 2]; yi[0] += 1
                if blk == 0:
                    src = PS[6][:, :].rearrange("p (j c) -> p c j", j=T)
                    cp("dve", ys.rearrange("p (c j) -> p c j", j=T), src, [bPS[6]], [by])
                else:
                    c0 = (blk - 1) * 32
                    for jb in range(4):
                        src = PS[2 + jb][:, :].rearrange("p (j c) -> p c j", j=4)[:, c0:c0 + 32, :]
                        dst = ys.rearrange("p (c j) -> p c j", j=T)[:, :, jb * 4:(jb + 1) * 4]
                        if jb % 2 == 0:
                            cp("dve", dst, src, [bPS[2 + jb]], [by])
                        else:
                            act(dst, src, AF.Copy, [bPS[2 + jb]], [by])
                act(Y1, ys, AF.Square, [by], [bY1])
                ts("pool", Y1, Y1, 0.044715, 1.0, ALU.mult, ALU.add, [bY1], [bY1])
                tt("pool", Y1, Y1, ys, ALU.mult, [bY1, by], [bY1])
                act(Y1, Y1, AF.Sigmoid, [bY1], [bY1], scale=GC)
                tt("dve", gs_, Y1, ys, ALU.mult, [bY1, by], [bgs])
                ld("sp", DAP(GT, k * 128 * NT + blk * TB, [[NT, 128], [1, TB]]), gs_, [bgs], [bGT])
        prep2(0)
        for tau in range(T + 1):
            gen2(0, tau)
        for k in range(4):
            ld("sp", UK, DAP(UT, k * 128 * NT, [[NT, 128], [1, NT]]), [bUT, bUK], [bUK])
            if k + 1 < 4:
                prep2(k + 1)
            for j in range(T):
                if k + 1 < 4:
                    gen2(k + 1, j)
                    if j == T - 1:
                        gen2(k + 1, T)
                outj(k, j)
            evac(k)
    P.barrier()
    if ph(7):
        s5()
    P.barrier()

    def glu():
        WG = arb(0, 1024)
        GB = [arb(1024 + i * 1024, 1024) for i in range(2)]
        SG_ = [arf(3072 + i * 512, 512) for i in range(2)]
        OB = [arb(4096 + i * 256, 256) for i in range(2)]
        bWG = Buf(); bGB = [Buf(), Buf()]; bSG = [Buf(), Buf()]; bOB = [Buf(), Buf()]
        ld("pool", WG.rearrange("p (k c) -> p k c", c=512), DAP(w_glu, 0, [[512, 128], [128 * 512, 4], [1, 512]]), [], [bWG])
        oi = 0
        for b in range(NB):
            gb = GB[b % 2]; bg = bGB[b % 2]
            ld("sp", gb.rearrange("p (k t) -> p k t", t=512), DAP(GT, b * TB, [[NT, 128], [128 * NT, 4], [1, TB]]), [bGT], [bg])
            for m in range(4):
                pb_ = oi % 2
                for k in range(4):
                    mm(PS[pb_][:, :], WG[:, k * 512 + m * 128:k * 512 + (m + 1) * 128], gb[:, k * 512:(k + 1) * 512], k == 0, k == 3, [bWG, bg], [bPS[pb_]])
                sg = SG_[oi % 2]; bs = bSG[oi % 2]; ob = OB[oi % 2]; bo = bOB[oi % 2]; oi += 1
                act(sg, PS[pb_][:, :], AF.Sigmoid, [bPS[pb_]], [bs])
                tt("dve", ob, sg, gb[:, m * 512:(m + 1) * 512], ALU.mult, [bs, bg], [bo])
                ld("sp", DAP(MO, (4 + m) * 128 * NT + b * TB, [[NT, 128], [1, TB]]), ob, [bo], [bMOb[b]])
    if ph(8):
        glu()

    def outproj():
        WO = arb(4608, 4096)
        MB_ = [arb(8704 + i * 2048, 2048) for i in range(2)]
        bWO = Buf(); bMB_ = [Buf(), Buf()]
        ld("pool", WO.rearrange("p (k c) -> p k c", c=D), DAP(w_out, 0, [[D, 128], [128 * D, 8], [1, D]]), [], [bWO])
        for b in range(NB):
            s = 0 if b == 0 else 1
            tb = slice(b * TB, (b + 1) * TB)
            mb = MB_[b % 2]; bm = bMB_[b % 2]
            ld("sp", mb.rearrange("p (k t) -> p k t", t=512), DAP(MO, b * TB, [[NT, 128], [128 * NT, 8], [1, TB]]), [bMO, bMOb[b]], [bm])
            for m in range(8):
                pb_ = 2 + m % 4
                for k in range(8):
                    mm(PS[pb_][:, :], WO[:, k * D + m * 128:k * D + (m + 1) * 128], mb[:, k * 512:(k + 1) * 512], k == 0, k == 7, [bWO, bm], [bPS[pb_]])
                stt("dve", XR[:, m, tb], PS[pb_][:, :], GATE[:, 0, 0, m, s:s + 1], XR[:, m, tb], ALU.mult, ALU.add,
                    [bPS[pb_], bMODs[0], bXR[b]], [bXR[b]])
    if ph(9):
        outproj()
    if ph(10):
        norm_to_H(0, 1)
    P.barrier()

    if ph(10):
        mlp(0)

    if ph(11):
        norm_to_H(1, 0)
    P.barrier()

    def poolmix():
        LP = 2048 + 16
        PA = arf(0, LP); PB = arf(LP, LP); RCl = arf(2 * LP, 2048)
        o0 = 2 * LP + 2048
        RCc2 = arf(o0, 512); TMPP = [arf(o0 + 512, 512), arf(o0 + 1024, 512)]
        PWt = arb(o0 + 1536, 256)
        PSC = arf(o0 + 1792, 16)
        bPA = Buf(); bPB = Buf(); bRC = Buf(); bPW = Buf(); bPSC = Buf(); bTMPP = [Buf(), Buf()]
        ld("sp", PSC[:, 0:8], DAP(pool_scale, 0, [[1, 128], [128, 8]]), [], [bPSC], **SL)
        gp2 = nc.alloc_sbuf_tensor("GP2", [128, 8, 2], F32)
        tt("dve", gp2[:], GATE[:, 1, 0, :, :], PSC[:, 0:8].unsqueeze(2).to_broadcast([128, 8, 2]), ALU.mult, [bMODs[1], bPSC], [bPSC])
        ts("dve", gp2[:], gp2[:], -1.0, None, ALU.mult, None, [bPSC], [bPSC])

        def wsum(g, L):
            Lp = L + 16
            cur, bc, oth, bo = PA, bPA, PB, bPB
            tt("dve", oth[:, 1:Lp], cur[:, 0:Lp - 1], cur[:, 1:Lp], ALU.add, [bc], [bo])
            cur, bc, oth, bo = oth, bo, cur, bc
            lo, hi, sh = 1, Lp, 1
            for _ in range(g):
                nlo, nhi = lo + sh, hi - sh
                tt("dve", oth[:, nlo:nhi], cur[:, nlo - sh:nhi - sh], cur[:, nlo + sh:nhi + sh], ALU.add, [bc], [bo])
                cur, bc, oth, bo = oth, bo, cur, bc
                lo, hi, sh = nlo, nhi, sh * 2
            return cur, bc
        pc = [0]; ti_ = [0]
        for g in range(4):
            w = 2 ** (g + 1); half = w // 2
            ld("pool", PWt.rearrange("p (k c) -> p k c", c=256), DAP(pool_w, g * 65536, [[256, 128], [128 * 256, 2], [1, 256]]), [bPW], [bPW])
            L = 256
            ms("dve", PA[:, 0:L + 16], 0.0, [bPA]); ms("dve", PA[:, 8:8 + L], 1.0, [bPA])
            res, br = wsum(g, L)
            P.op("dve", lambda e, res=res: e.reciprocal(out=RCc2[:, 0:256], in_=res[:, 8:8 + 256]), reads=[br], writes=[bRC])
            cp("dve", RCc2[:, 256:512], RCc2[:, 0:256], [bRC], [bRC])
            cp("dve", RCl[:, 0:2048].rearrange("p (a c) -> p a c", c=16),
               RCc2[:, 120:136].unsqueeze(1).to_broadcast([128, 128, 16]), [bRC], [bRC])
            cp("dve", RCl[:, 0:8], RCc2[:, 0:8], [bRC], [bRC])
            cp("dve", RCl[:, 2040:2048], RCc2[:, 248:256], [bRC], [bRC])
            for b in range(NB):
                s_ = 0 if b == 0 else 1
                t0 = b * TB
                tb = slice(t0, t0 + TB)
                if b == 0:
                    segs = ((0, 256, True, True), (256, 512, True, True))
                    rc = RCc2[:, 0:512]
                else:
                    segs = ((0, 512, b == 1, b == NB - 1),)
                    rc = RCl[:, (b - 1) * TB:b * TB]
                hdeps = [bH[x] for x in (b - 1, b, b + 1) if 0 <= x < NB]
                for mm_ in range(2):
                    m = 2 * g + mm_
                    pa_ = (2 * pc[0]) % 6; pb_ = (2 * pc[0] + 1) % 6; pc[0] += 1
                    lw = [PWt[:, kk * 256 + mm_ * 128:kk * 256 + (mm_ + 1) * 128] for kk in range(2)]
                    jobs = [(0, 0, 512)]
                    for sft in list(range(-half, 0)) + list(range(1, half)):
                        for (lo, hi, le, re_) in segs:
                            c0 = lo + (-sft if (sft < 0 and le) else 0)
                            c1 = hi - (sft if (sft > 0 and re_) else 0)
                            jobs.append((sft, c0, c1))
                    nj = len(jobs)
                    for ji, (sft, c0, c1) in enumerate(jobs):
                        for kk in range(2):
                            mm(PS[pa_][:, c0:c1], lw[kk], H[:, 2 * g + kk, t0 + c0 + sft:t0 + c1 + sft],
                               ji == 0 and kk == 0, ji == nj - 1 and kk == 1, [bPW] + hdeps, [bPS[pa_]])
                    for kk in range(2):
                        mm(PS[pb_][:, :], lw[kk], H[:, 2 * g + kk, tb], kk == 0, kk == 1, [bPW, bH[b]], [bPS[pb_]])
                    tp = TMPP[ti_[0] % 2]; btp = bTMPP[ti_[0] % 2]; ti_[0] += 1
                    tt("dve", tp, PS[pa_][:, :], rc, ALU.mult, [bPS[pa_], bRC], [btp])
                    tt("dve", tp, PS[pb_][:, :], tp, ALU.subtract, [bPS[pb_], btp], [btp])
                    stt("dve", XR[:, m, tb], tp, gp2[:, m, s_:s_ + 1], XR[:, m, tb], ALU.mult, ALU.add, [btp, bPSC, bXR[b]], [bXR[b]])
    if ph(11):
        poolmix()
    if ph(12):
        norm_to_H(1, 1)
    P.barrier()

    if ph(12):
        mlp(1)

    def final():
        YB = arf(10752, 4096)
        YT = [arf(14848 + i * 1024, 1024) for i in range(2)]
        bYB = Buf(); bYT = [Buf(), Buf()]
        ti = 0
        for b in range(NB):
            norm_block(b, lambda k: NG[:, 4, k:k + 1], lambda k: None, lambda k: YB[:, k * 512:(k + 1) * 512], [bYB])
            for t4 in range(4):
                yt = YT[ti % 2]; by = bYT[ti % 2]
                pa, pb2 = (0, 1) if ti % 2 == 0 else (2, 3)
                ti += 1
                for k in range(8):
                    pp = pa if k < 4 else pb2
                    tr(PS[pp][:, (k % 4) * 128:(k % 4 + 1) * 128], YB[:, k * 512 + t4 * 128:k * 512 + (t4 + 1) * 128], ident[:], [bYB, bC], [bPS[pp]])
                cp("dve", yt[:, 0:512], PS[pa][:, :], [bPS[pa]], [by])
                act(yt[:, 512:1024], PS[pb2][:, :], AF.Copy, [bPS[pb2]], [by])
                tglob = b * 4 + t4
                if tglob < 4:
                    dst = DAP(yc, tglob * 128 * D, [[D, 128], [1, D]])
                else:
                    dst = DAP(yl, (tglob - 4) * 128 * D, [[D, 128], [1, D]])
                out_events.append(ld("sp", dst, yt, [by], []))
    if ph(13):
        final()

    P.wait_all_on("sp", out_events)
    P.emit()
    return nc


_NC = None
_DEBUG_HOOK = None


def _consts():
    ident = np.eye(128, dtype=np.float32)
    maskq = np.kron(np.eye(4, dtype=np.float32), np.ones((32, 32), np.float32))
    t = np.arange(2048)
    posr = (t // 64).astype(np.float32)
    posc = (t % 64).astype(np.float32)
    fi = np.full((128, 2), 1.0e4, np.float32)
    for p in range(128):
        dd = p % 64
        if dd < 32:
            fi[p, 0] = dd % 16
        else:
            fi[p, 1] = (dd - 32) % 16
    return dict(c_ident=ident, c_maskq=maskq, c_posr=posr, c_posc=posc, c_fi=fi)


def kernel(x_prompt, x_sample, cache_k, cache_v, state_s5, c, c_ctx, mod_w, mod_b, norm_g,
           mix_w_in, mix_w_out, diff_lambda_qk, diff_subln_g, s5_lambda_re, s5_lambda_im, s5_log_dt,
           s5_b_re, s5_b_im, s5_c_re, s5_c_im, s5_d, s5_w_glu, pool_w, pool_scale, ff_w1, ff_w2,
           final_norm_g):
    global _NC
    f = lambda a: np.ascontiguousarray(np.asarray(a, dtype=np.float32))
    if _NC is None:
        _NC = build()
    nc = _NC
    cst = _consts()
    shared = dict(mod_w=f(mod_w), mod_b=f(mod_b), norm_g=f(norm_g), w_in=f(mix_w_in)[0], w_out=f(mix_w_out)[0],
                  lam_qk=f(diff_lambda_qk)[0], subln=f(diff_subln_g)[0], lam_re=f(s5_lambda_re)[0], lam_im=f(s5_lambda_im)[0],
                  log_dt=f(s5_log_dt)[0], b_re=f(s5_b_re)[0], b_im=f(s5_b_im)[0], c_re=f(s5_c_re)[0], c_im=f(s5_c_im)[0],
                  s5_d=f(s5_d)[0], w_glu=f(s5_w_glu)[0], pool_w=f(pool_w)[0], pool_scale=f(pool_scale)[0],
                  ff_w1=f(ff_w1), ff_w2=f(ff_w2), fng=f(final_norm_g), **cst)
    xp = f(x_prompt); xs = f(x_sample); ckk = f(cache_k); cvv = f(cache_v); sst = f(state_s5); cc = f(c); cctx = f(c_ctx)
    in_maps = []
    for core in range(8):
        ls = core // 2
        m = dict(shared)
        m["xc"] = np.ascontiguousarray(xp[2 * core:2 * core + 2].reshape(512, D))
        m["xl"] = np.ascontiguousarray(xs[ls])
        m["ck"] = np.ascontiguousarray(ckk[ls, 0].reshape(512, 512))
        m["cv"] = np.ascontiguousarray(cvv[ls, 0].reshape(512, 512))
        m["st"] = np.ascontiguousarray(sst[ls, 0])
        m["cvec"] = np.ascontiguousarray(np.stack([cctx, cc[ls]], axis=0))
        in_maps.append(m)
    if _DEBUG_HOOK is not None:
        return _DEBUG_HOOK(nc, in_maps)
    res = run_bass_kernel_spmd(nc, in_maps, core_ids=list(range(8)))
    R = res.results
    y_prompt = np.concatenate([R[i]["yc"].reshape(2, 256, D) for i in range(8)], axis=0)
    y_sample = np.stack([np.concatenate([R[2 * s]["yl"][:1024], R[2 * s + 1]["yl"][1024:]], axis=0) for s in range(4)], axis=0)
    nk = np.concatenate([R[i]["nk"].reshape(2, 1, 256, 4, 128) for i in range(8)], axis=0)
    nv = np.concatenate([R[i]["nv"].reshape(2, 1, 256, 4, 128) for i in range(8)], axis=0)
    nst = np.concatenate([R[i]["ns"].reshape(2, 1, 2, 32, 64, 2) for i in range(8)], axis=0)
    return (y_prompt.astype(np.float32), y_sample.astype(np.float32), nk.astype(np.float32),
            nv.astype(np.float32), nst.astype(np.float32))
```

```python
import math
import os
import numpy as np
import concourse.bass as bass
import concourse.mybir as mybir
from concourse.bass_utils import run_bass_kernel_spmd

F32 = mybir.dt.float32
BF16 = mybir.dt.bfloat16
ALU = mybir.AluOpType
AF = mybir.ActivationFunctionType

ENGS = ("pe", "act", "dve", "pool", "sp")
NPOOL = 20


DEBUG_TAGS = None


def _tagged(fn, tag):
    def g(e):
        ins = fn(e)
        DEBUG_TAGS[ins.ins.name] = tag
        return ins
    return g


class Buf:
    __slots__ = ("name", "w", "r")

    def __init__(self, name=""):
        self.name = name
        self.w = None
        self.r = []


class Prog:
    def __init__(self, nc):
        self.nc = nc
        self.ops = {e: [] for e in ENGS}
        self.cnt = {e: 0 for e in ("pe", "act", "dve", "pool")}
        self.sem = {}
        for e in ("pe", "act", "dve", "pool"):
            self.sem[e] = nc.alloc_semaphore(name="s_" + e)
        self.dq = {}
        for q in ("sp", "pool", "act"):
            self.dq[q] = 0
            for i in range(NPOOL):
                self.sem[("d", q, i)] = nc.alloc_semaphore(name=f"d_{q}{i}")
        self.known = {e: {} for e in ENGS}

    def _collect(self, eng, reads, writes):
        waits = {}

        def need(ev):
            if ev is None:
                return
            k, v, pe = ev
            if eng == "pe" and pe == "pe":
                return
            if waits.get(k, 0) < v:
                waits[k] = v
        for b in reads:
            need(b.w)
        for b in writes:
            need(b.w)
            for ev in b.r:
                need(ev)
        out = []
        kn = self.known[eng]
        for k, v in waits.items():
            if kn.get(k, 0) >= v:
                continue
            kn[k] = v
            out.append((k, v))
        return out

    def _record(self, ev, reads, writes):
        for b in writes:
            b.w = ev
            b.r = []
        for b in reads:
            b.r = [e for e in b.r if e[0] != ev[0]] + [ev]

    def op(self, eng, fn, reads=(), writes=()):
        if DEBUG_TAGS is not None:
            import traceback
            st_ = traceback.extract_stack(limit=4)
            fn = _tagged(fn, " <- ".join(f"{f.lineno}" for f in st_[:-1]))
        waits = self._collect(eng, reads, writes)
        self.cnt[eng] += 1
        ev = (eng, self.cnt[eng], eng)
        self._record(ev, reads, writes)
        self.ops[eng].append((waits, fn, (eng, 1)))
        return ev

    def dma(self, q, fn, reads=(), writes=()):
        n = self.dq[q]
        self.dq[q] += 1
        slot, rnd = n % NPOOL, n // NPOOL
        key = ("d", q, slot)
        waits = self._collect(q, reads, writes)
        if rnd > 0:
            kn = self.known[q]
            if kn.get(key, 0) < 16 * rnd:
                kn[key] = 16 * rnd
                waits.append((key, 16 * rnd))
        ev = (key, 16 * (rnd + 1), "dma")
        self._record(ev, reads, writes)
        self.ops[q].append((waits, fn, (key, 16)))
        return ev

    def barrier(self):
        evs = [(e, self.cnt[e], e) for e in self.cnt if self.cnt[e] > 0]
        for q in self.dq:
            n = self.dq[q]
            for slot in range(min(n, NPOOL)):
                last = ((n - 1 - slot) // NPOOL) * NPOOL + slot
                rnd = last // NPOOL
                evs.append((("d", q, slot), 16 * (rnd + 1), "dma"))
        for e in ENGS:
            self.wait_all_on(e, evs)

    def wait_all_on(self, eng, events):
        waits = []
        kn = self.known[eng]
        for k, v, _ in events:
            if kn.get(k, 0) < v:
                kn[k] = v
                waits.append((k, v))
        self.ops[eng].append((waits, None, None))

    def emit(self):
        nc = self.nc
        engobj = {"pe": "tensor", "act": "scalar", "dve": "vector", "pool": "gpsimd", "sp": "sync"}
        with nc.Block() as block:
            for e in ENGS:
                ops = self.ops[e]
                if not ops:
                    continue

                def body(eng, ops=ops):
                    for waits, fn, inc in ops:
                        for k, v in waits:
                            eng.wait_ge(self.sem[k], v)
                        if fn is not None:
                            fn(eng).then_inc(self.sem[inc[0]], inc[1])
                getattr(block, engobj[e])(body)


D = 1024
NT = 2560
NB = 5
TB = 512
T = 16
NCL = 2048 // T
NCC = 512 // T
EPS = 1e-6
LAM_INIT0 = 0.8 - 0.6 * math.exp(0.0)
GC = 1.5957691216057308


def AP(t, off, dims, p0=0, p1=128):
    base = t[p0:p1]
    return bass.AP(base.tensor, base.offset + off, [list(base.ap[0])] + [list(d) for d in dims])


def DAP(t, off, dims):
    return bass.AP(t, off, [list(d) for d in dims])


STOP = 99
SUB = 99
SUBB = 99


def build():
    ph = lambda n: n <= STOP
    nc = bass.Bass("TRN2", target_bir_lowering=False)
    P = Prog(nc)

    def din(name, shape):
        return nc.dram_tensor(name, list(shape), F32, kind="ExternalInput")

    def dout(name, shape):
        return nc.dram_tensor(name, list(shape), F32, kind="ExternalOutput")

    xc = din("xc", [512, D]); xl = din("xl", [2048, D])
    ck = din("ck", [512, 512]); cv = din("cv", [512, 512])
    st = din("st", [2, 32, 64, 2]); cvec = din("cvec", [2, D])
    mod_w = din("mod_w", [2, D, 6 * D]); mod_b = din("mod_b", [2, 6 * D]); norm_g = din("norm_g", [2, 2, D])
    w_in = din("w_in", [D, 2048]); w_out = din("w_out", [D, D])
    lam_qk = din("lam_qk", [4, 64]); subln = din("subln", [128])
    lam_re = din("lam_re", [2, 32, 64]); lam_im = din("lam_im", [2, 32, 64]); log_dt = din("log_dt", [2, 32])
    b_re = din("b_re", [2, 32, 64, 16]); b_im = din("b_im", [2, 32, 64, 16])
    c_re = din("c_re", [2, 32, 16, 64]); c_im = din("c_im", [2, 32, 16, 64])
    s5_d = din("s5_d", [512]); w_glu = din("w_glu", [512, 512])
    pool_w = din("pool_w", [4, 256, 256]); pool_scale = din("pool_scale", [D])
    ff_w1 = din("ff_w1", [2, D, 4 * D]); ff_w2 = din("ff_w2", [2, 4 * D, D]); fng = din("fng", [D])
    c_ident = din("c_ident", [128, 128]); c_maskq = din("c_maskq", [128, 128])
    c_posr = din("c_posr", [2048]); c_posc = din("c_posc", [2048]); c_fi = din("c_fi", [128, 2])

    yc = dout("yc", [512, D]); yl = dout("yl", [2048, D])
    nk = dout("nk", [512, 512]); nv = dout("nv", [512, 512]); ns = dout("ns", [2, 2, 32, 64, 2])

    def dscr(name, shape):
        return nc.dram_tensor(name, list(shape), BF16, kind="Internal")
    QT = dscr("QT", [4, 128, NT]); KT = dscr("KT", [4, 128, 3072]); VS = dscr("VS", [3072, 512])
    UT = dscr("UT", [4, 128, NT]); MO = dscr("MO", [8, 128, NT]); GT = dscr("GT", [4, 128, NT])
    DBG = nc.dram_tensor("DBG", [512, 512], F32, kind="Internal")
    bQT, bKT, bVS, bUT, bMO, bGT = [Buf(n) for n in ("QT", "KT", "VS", "UT", "MO", "GT")]
    bQTb = [Buf() for _ in range(NB)]; bUTb = [Buf() for _ in range(NB)]; bMOb = [Buf() for _ in range(NB)]
    out_events = []

    XR = nc.alloc_sbuf_tensor("XR", [128, 8, NT], F32)
    H = nc.alloc_sbuf_tensor("H", [128, 8, NT], BF16)
    bXR = [Buf(f"XR{b}") for b in range(NB)]
    bH = [Buf(f"H{b}") for b in range(NB)]
    ARN = 20480
    AR = nc.alloc_sbuf_tensor("AR", [128, ARN], F32)
    ident = nc.alloc_sbuf_tensor("ident", [128, 128], F32)
    identb = nc.alloc_sbuf_tensor("identb", [128, 128], BF16)
    onesb = nc.alloc_sbuf_tensor("onesb", [128, 128], BF16)
    maskq = nc.alloc_sbuf_tensor("maskq", [128, 128], F32)
    MODT = nc.alloc_sbuf_tensor("MODT", [128, 2, 48, 2], F32)
    NG = nc.alloc_sbuf_tensor("NG", [128, 5, 8], F32)
    GS = nc.alloc_sbuf_tensor("GS", [128, 2, 2, 2, 8, 2], F32)
    GATE = nc.alloc_sbuf_tensor("GATE", [128, 2, 2, 8, 2], F32)
    SMALL = nc.alloc_sbuf_tensor("SMALL", [128, 64], F32)
    RSTD = nc.alloc_sbuf_tensor("RSTD", [128, TB], F32)
    bC = Buf("consts"); bMODs = [Buf("mod0"), Buf("mod1")]; bSMALL = Buf("small"); bRSTD = Buf("rstd")

    PS = [nc.alloc_psum_tensor(f"ps{i}", [128, 512], F32) for i in range(8)]
    bPS = [Buf(f"ps{i}") for i in range(8)]

    def arf(off, n):
        return AR[:, off:off + n]

    def arb(off, n):
        return AR[:, off:off + n].bitcast(BF16)

    def act(out, in_, func, R, W, **kw):
        return P.op("act", lambda e: e.activation(out=out, in_=in_, func=func, **kw), reads=R, writes=W)

    def tt(eng, out, a, b, op, R, W):
        return P.op(eng, lambda e: e.tensor_tensor(out=out, in0=a, in1=b, op=op), reads=R, writes=W)

    def ts(eng, out, a, s1, s2, op0, op1, R, W):
        if s2 is None:
            return P.op(eng, lambda e: e.tensor_scalar(out=out, in0=a, scalar1=s1, scalar2=None, op0=op0), reads=R, writes=W)
        return P.op(eng, lambda e: e.tensor_scalar(out=out, in0=a, scalar1=s1, scalar2=s2, op0=op0, op1=op1), reads=R, writes=W)

    def stt(eng, out, a, s, b, op0, op1, R, W):
        return P.op(eng, lambda e: e.scalar_tensor_tensor(out=out, in0=a, scalar=s, in1=b, op0=op0, op1=op1), reads=R, writes=W)

    def cp(eng, out, in_, R, W):
        return P.op(eng, lambda e: e.tensor_copy(out=out, in_=in_), reads=R, writes=W)

    def ms(eng, out, val, W):
        return P.op(eng, lambda e: e.memset(out, val), writes=W)

    def mm(out, lhsT, rhs, start, stop, R, W, **kw):
        return P.op("pe", lambda e: e.matmul(out, lhsT=lhsT, rhs=rhs, start=start, stop=stop, **kw), reads=R, writes=W)

    def tr(out, in_, idn, R, W):
        return P.op("pe", lambda e: e.transpose(out, in_, idn), reads=R, writes=W)

    def ld(q, out, in_, R, W, **kw):
        return P.dma(q, lambda e: e.dma_start(out=out, in_=in_, **kw), reads=R, writes=W)

    SL = dict(allow_slow_non_contiguous=True)

    ld("sp", ident[:], c_ident.ap(), [], [bC])
    ld("sp", maskq[:], c_maskq.ap(), [], [bC])
    cp("dve", identb[:], ident[:], [bC], [bC])
    ms("dve", onesb[:], 1.0, [bC])
    for i in range(4):
        ld("sp", NG[:, i, :], DAP(norm_g, i * D, [[1, 128], [128, 8]]), [], [bC], **SL)
    ld("sp", NG[:, 4, :], DAP(fng, 0, [[1, 128], [128, 8]]), [], [bC], **SL)

    WI_pre = arb(0, 8192)
    bWI_pre = Buf("wi")
    ld("pool", WI_pre.rearrange("p (k c) -> p k c", c=2048), DAP(w_in, 0, [[2048, 128], [128 * 2048, 8], [1, 2048]]), [], [bWI_pre])
    XT = [arf(8192 + i * 1024, 1024) for i in range(4)]
    bXT = [Buf() for _ in range(4)]
    for ti in range(20):
        src = xc if ti < 4 else xl
        r0 = ti * 128 if ti < 4 else (ti - 4) * 128
        b = ti // 4
        xt = XT[ti % 4]; bx = bXT[ti % 4]
        ld("sp", xt, DAP(src, r0 * D, [[D, 128], [1, D]]), [], [bx])
        pa, pb = (4, 5) if ti % 2 == 0 else (6, 7)
        for k in range(8):
            pst = PS[pa if k < 4 else pb]; bp = bPS[pa if k < 4 else pb]
            tr(pst[:, (k % 4) * 128:(k % 4 + 1) * 128], xt[:, k * 128:(k + 1) * 128], ident[:], [bx, bC], [bp])
        for half in range(2):
            pst = PS[pa if half == 0 else pb]; bp = bPS[pa if half == 0 else pb]
            o = XR[:, half * 4:half * 4 + 4, ti * 128:(ti + 1) * 128]
            i_ = pst[:, :].rearrange("p (k t) -> p k t", t=128)
            if half == 0:
                cp("dve", o, i_, [bp], [bXR[b]])
            else:
                act(o, i_, AF.Copy, [bp], [bXR[b]])

    SCf = nc.alloc_sbuf_tensor("SCf", [128, 16], F32)
    SCbT = nc.alloc_sbuf_tensor("SCb", [128, 16], BF16)
    SCb = SCbT[:, :]
    MBT = nc.alloc_sbuf_tensor("MBT", [128, 2, 48], F32)
    bSC = Buf("sc"); bMB = Buf("mb")
    for s_ in range(2):
        ld("sp", SCf[:, :].rearrange("p (k s) -> p k s", s=2)[:, :, s_], DAP(cvec, s_ * D, [[1, 128], [128, 8]]), [], [bSC], **SL)
    act(SCb, SCf[:, :], AF.Silu, [bSC], [bSC])
    for l in range(2):
        ld("sp", MBT[:, l, :], DAP(mod_b, l * 6 * D, [[1, 128], [128, 48]]), [], [bMB], **SL)
    mci = [0]

    def mod_chunk(l, cc, MW, bMW, bank=0, per_chunk=False):
        w = MW[mci[0] % 2]; bw = bMW[mci[0] % 2]; mci[0] += 1
        ld("pool", w.rearrange("p (k c) -> p k c", c=512),
           DAP(mod_w, l * D * 6 * D + cc * 512, [[6 * D, 128], [128 * 6 * D, 8], [1, 512]]), [], [bw])
        for t4 in range(4):
            tt_i = cc * 4 + t4
            c0 = (t4 * 2) if per_chunk else (tt_i * 2)
            for k in range(8):
                mm(PS[bank][:, c0:c0 + 2], w[:, k * 512 + t4 * 128:k * 512 + (t4 + 1) * 128],
                   SCb[:, k * 2:k * 2 + 2], k == 0, k == 7, [bw, bSC], [bPS[bank]])
        if per_chunk:
            tt("dve", MODT[:, l, cc * 4:(cc + 1) * 4, :], PS[bank][:, 0:8].rearrange("p (t s) -> p t s", s=2),
               MBT[:, l, cc * 4:(cc + 1) * 4].unsqueeze(2).to_broadcast([128, 4, 2]), ALU.add, [bPS[bank], bMB], [bMODs[l]])

    def mod_finish(l, evac=True):
        if evac:
            tt("dve", MODT[:, l, :, :], PS[0][:, 0:96].rearrange("p (t s) -> p t s", s=2),
               MBT[:, l, :].unsqueeze(2).to_broadcast([128, 48, 2]), ALU.add, [bPS[0], bMB], [bMODs[l]])
        for n in range(2):
            sh = MODT[:, l, (3 * n) * 8:(3 * n) * 8 + 8, :]
            sc = MODT[:, l, (3 * n + 1) * 8:(3 * n + 1) * 8 + 8, :]
            gt = MODT[:, l, (3 * n + 2) * 8:(3 * n + 2) * 8 + 8, :]
            g = NG[:, l * 2 + n, :].unsqueeze(2).to_broadcast([128, 8, 2])
            stt("dve", GS[:, l, n, 0, :, :], sc, 1.0, g, ALU.add, ALU.mult, [bMODs[l], bC], [bMODs[l]])
            cp("dve", GS[:, l, n, 1, :, :], sh, [bMODs[l]], [bMODs[l]])
            cp("dve", GATE[:, l, n, :, :], gt, [bMODs[l]], [bMODs[l]])
    MW0 = [arb(12288 + i * 2048, 2048) for i in range(2)]
    bMW0 = [Buf("mw0"), Buf("mw1")]
    for cc in range(12):
        mod_chunk(0, cc, MW0, bMW0)
    mod_finish(0)

    SQ = arb(ARN - 2048, 2048)
    TMPNS = [arf(ARN - 2048 - 512, 512), arf(ARN - 2048 - 1024, 512)]
    bSQ = Buf("sq"); bTMPNS = [Buf("tmpn0"), Buf("tmpn1")]
    tni = [0]

    def norm_block(b, gs_fn, sh_fn, out_fn, outbufs, psb=7, bMOD=bC):
        tb = slice(b * TB, (b + 1) * TB)
        for k in range(8):
            act(SQ[:, k * 512:(k + 1) * 512], XR[:, k, tb], AF.Square, [bXR[b]], [bSQ])
        for k in range(8):
            mm(PS[psb][:, :], onesb[:], SQ[:, k * 512:(k + 1) * 512], k == 0, k == 7, [bSQ, bC], [bPS[psb]])
        act(RSTD[:], PS[psb][:, :], AF.Ln, [bPS[psb]], [bRSTD], scale=1.0 / D, bias=EPS)
        act(RSTD[:], RSTD[:], AF.Exp, [bRSTD], [bRSTD], scale=-0.5)
        for k in range(8):
            sh = sh_fn(k)
            if sh is None:
                stt("dve", out_fn(k), XR[:, k, tb], gs_fn(k), RSTD[:], ALU.mult, ALU.mult, [bXR[b], bRSTD, bMOD, bC], outbufs)
            else:
                TMPN = TMPNS[tni[0] % 2]; bTMPN = bTMPNS[tni[0] % 2]; tni[0] += 1
                stt("dve", TMPN, XR[:, k, tb], gs_fn(k), RSTD[:], ALU.mult, ALU.mult, [bXR[b], bRSTD, bMOD], [bTMPN])
                act(out_fn(k), TMPN, AF.Identity, [bTMPN, bMOD], outbufs, bias=sh, scale=1.0)

    def norm_to_H(l, n):
        for b in range(NB):
            s = 0 if b == 0 else 1
            norm_block(b, lambda k: GS[:, l, n, 0, k, s:s + 1], lambda k: GS[:, l, n, 1, k, s:s + 1],
                       lambda k: H[:, k, b * TB:(b + 1) * TB], [bH[b]], bMOD=bMODs[l])

    def mlp(l):
        W1 = [arb(i * 2048, 2048) for i in range(2)]
        W2 = [arb(4096 + i * 2048, 2048) for i in range(2)]
        RL = [arb(8192 + i * 256, 256) for i in range(2)]
        AA = [arb(8704 + i * 1024, 1024) for i in range(2)]
        bW1 = [Buf(), Buf()]; bW2 = [Buf(), Buf()]; bRL = [Buf(), Buf()]; bAA = [Buf(), Buf()]
        ri = [0]; pi = [0]

        def load_w(fg):
            w1 = W1[fg % 2]; w2 = W2[fg % 2]
            ld("pool", w1.rearrange("p (k c) -> p k c", c=512),
               DAP(ff_w1, l * D * 4 * D + fg * 512, [[4 * D, 128], [128 * 4 * D, 8], [1, 512]]), [], [bW1[fg % 2]])
            ld("pool", w2.rearrange("p (f c) -> p f c", c=D),
               DAP(ff_w2, l * 4 * D * D + fg * 512 * D, [[D, 128], [128 * D, 4], [1, D]]), [], [bW2[fg % 2]])

        def phase1(step):
            fg, b = divmod(step, NB)
            w1 = W1[fg % 2]
            tb = slice(b * TB, (b + 1) * TB)
            aa = AA[step % 2]; ba = bAA[step % 2]
            for f in range(4):
                pb_ = pi[0] % 3; pi[0] += 1
                for k in range(8):
                    mm(PS[pb_][:, :], w1[:, k * 512 + f * 128:k * 512 + (f + 1) * 128], H[:, k, tb], k == 0, k == 7,
                       [bW1[fg % 2], bH[b]], [bPS[pb_]])
                rl = RL[ri[0] % 2]; br = bRL[ri[0] % 2]; ri[0] += 1
                act(rl, PS[pb_][:, :], AF.Relu, [bPS[pb_]], [br])
                tt("pool", aa[:, f * 512:(f + 1) * 512], rl, rl, ALU.mult, [br], [ba])

        def phase2(step):
            fg, b = divmod(step, NB)
            w2 = W2[fg % 2]
            s_ = 0 if b == 0 else 1
            tb = slice(b * TB, (b + 1) * TB)
            aa = AA[step % 2]; ba = bAA[step % 2]
            for m in range(8):
                pb_ = 3 + (m % 4)
                for f in range(4):
                    mm(PS[pb_][:, :], w2[:, f * D + m * 128:f * D + (m + 1) * 128], aa[:, f * 512:(f + 1) * 512], f == 0, f == 3,
                       [bW2[fg % 2], ba], [bPS[pb_]])
                stt("dve", XR[:, m, tb], PS[pb_][:, :], GATE[:, l, 1, m, s_:s_ + 1], XR[:, m, tb], ALU.mult, ALU.add,
                    [bPS[pb_], bMODs[l], bXR[b]], [bXR[b]])
        nsteps = 8 * NB
        load_w(0)
        phase1(0)
        for step in range(nsteps):
            if step + 1 < nsteps:
                if (step + 1) % NB == 0:
                    load_w((step + 1) // NB)
                phase1(step + 1)
            phase2(step)

    LQ = SMALL[0:1, 0:4 * 64]  if False else None
    LQK = nc.alloc_sbuf_tensor("LQK", [1, 256], F32)
    NLAM = nc.alloc_sbuf_tensor("NLAM", [128, 2], F32)
    SGs = nc.alloc_sbuf_tensor("SGs", [128, 2], F32)
    ones1 = nc.alloc_sbuf_tensor("ones1", [1, 128], F32)
    bL = Buf("lam")
    ld("sp", LQK[:], DAP(lam_qk, 0, [[0, 1], [1, 256]]), [], [bL])
    ms("dve", ones1[:], 1.0, [bL])
    ms("dve", SMALL[0:1, 0:8], 0.0, [bSMALL])
    tt("dve", LQK[0:1, 0:64], LQK[0:1, 0:64], LQK[0:1, 64:128], ALU.mult, [bL], [bL])
    tt("dve", LQK[0:1, 128:192], LQK[0:1, 128:192], LQK[0:1, 192:256], ALU.mult, [bL], [bL])
    P.op("dve", lambda e: e.reduce_sum(out=SMALL[0:1, 0:1], in_=LQK[0:1, 0:64], axis=mybir.AxisListType.X), reads=[bL], writes=[bSMALL])
    P.op("dve", lambda e: e.reduce_sum(out=SMALL[0:1, 1:2], in_=LQK[0:1, 128:192], axis=mybir.AxisListType.X), reads=[bL], writes=[bSMALL])
    act(SMALL[0:1, 0:2], SMALL[0:1, 0:2], AF.Exp, [bSMALL], [bSMALL])
    tt("dve", SMALL[0:1, 2:3], SMALL[0:1, 1:2], SMALL[0:1, 0:1], ALU.subtract, [bSMALL], [bSMALL])
    ts("dve", SMALL[0:1, 2:3], SMALL[0:1, 2:3], -LAM_INIT0, None, ALU.add, None, [bSMALL], [bSMALL])
    mm(PS[6][:, 0:1], ones1[0:1, :], SMALL[0:1, 2:3], True, True, [bL, bSMALL], [bPS[6]])
    cp("dve", NLAM[:, 0:1], PS[6][:, 0:1], [bPS[6]], [bL])
    ld("sp", SGs[:, 0:1], DAP(subln, 0, [[1, 128], [1, 1]]), [], [bL])
    ts("dve", SGs[:, 1:2], SGs[:, 0:1], 1.0 - LAM_INIT0, None, ALU.mult, None, [bL], [bL])

    if ph(3):
        norm_to_H(0, 0)

    def inproj():
        WI = arb(0, 8192)
        COS = arf(8192, 2048); SIN = arf(10240, 2048)
        POSR = arf(12288, 2048)
        PERM = arf(14336, 128)
        FI = arf(14464, 2); INV = arf(14466, 2)
        bWI = bWI_pre; bROPE = Buf(); bTMP = Buf()
        ld("sp", FI, c_fi.ap(), [], [bROPE])
        act(INV, FI, AF.Exp, [bROPE], [bROPE], scale=-math.log(10000.0) / 16.0)
        ld("sp", POSR, DAP(c_posr, 0, [[0, 128], [1, 2048]]), [], [bTMP])
        ts("dve", COS, POSR, INV[:, 0:1], None, ALU.mult, None, [bTMP, bROPE], [bROPE])
        ld("sp", POSR, DAP(c_posc, 0, [[0, 128], [1, 2048]]), [bTMP], [bTMP])
        stt("dve", COS, POSR, INV[:, 1:2], COS, ALU.mult, ALU.add, [bTMP, bROPE], [bROPE])
        ts("dve", SIN, COS, 1.0 / (2 * math.pi), 12582912.0, ALU.mult, ALU.add, [bROPE], [bROPE])
        ts("dve", SIN, SIN, -12582912.0, None, ALU.add, None, [bROPE], [bROPE])
        stt("dve", SIN, COS, 1.0 / (2 * math.pi), SIN, ALU.mult, ALU.subtract, [bROPE], [bROPE])
        ts("dve", COS, SIN, 0.25, 12582912.0, ALU.add, ALU.add, [bROPE], [bROPE])
        ts("dve", COS, COS, -12582912.0, None, ALU.add, None, [bROPE], [bROPE])
        stt("dve", COS, SIN, 0.25, COS, ALU.add, ALU.subtract, [bROPE], [bROPE])
        act(COS, COS, AF.Sin, [bROPE], [bROPE], scale=2 * math.pi)
        act(SIN, SIN, AF.Sin, [bROPE], [bROPE], scale=2 * math.pi)
        ms("dve", PERM, 0.0, [bROPE])
        for base in range(0, 128, 32):
            ts("dve", PERM[:, base:base + 16], ident[:, base + 16:base + 32], -1.0, None, ALU.mult, None, [bC], [bROPE])
            cp("dve", PERM[:, base + 16:base + 32], ident[:, base:base + 16], [bC], [bROPE])
        QF = [arf(14592 + i * 512, 512) for i in range(2)]
        T1 = arf(15616, 512)
        STG = [arb(16128 + i * 256, 256) for i in range(4)]
        VF = [arf(12288 + i * 512, 512) for i in range(2)]
        VB = [arb(13312 + i * 256, 256) for i in range(2)]
        bQF = [Buf(), Buf()]; bT1 = Buf(); bSTG = [Buf() for _ in range(4)]; bVF = [Buf(), Buf()]; bVB = [Buf(), Buf()]
        si = 0; qi = 0; vi = 0; pi = 0
        if SUB < 1:
            return
        for b in range(NB):
            if b >= SUBB:
                break
            tb = slice(b * TB, (b + 1) * TB)
            for ot in list(range(0, 8)) + list(range(12, 16)):
                pb_ = pi % 2; pi += 1
                for k in range(8):
                    mm(PS[pb_][:, :], WI[:, k * 2048 + ot * 128:k * 2048 + (ot + 1) * 128], H[:, k, tb], k == 0, k == 7,
                       [bWI, bH[b]], [bPS[pb_]])
                stg = STG[si % 4]; bs = bSTG[si % 4]; si += 1
                if ot < 8 and b >= 1:
                    lt = slice((b - 1) * TB, b * TB)
                    qf = QF[qi % 2]; bq = bQF[qi % 2]; qi += 1
                    act(qf, PS[pb_][:, :], AF.Copy, [bPS[pb_]], [bq])
                    mm(PS[2][:, :], PERM, qf, True, True, [bROPE, bq], [bPS[2]])
                    tt("dve", T1, qf, COS[:, lt], ALU.mult, [bq, bROPE], [bT1])
                    tt("dve", qf, PS[2][:, :], SIN[:, lt], ALU.mult, [bPS[2], bROPE, bq], [bq])
                    tt("pool", stg, T1, qf, ALU.add, [bT1, bq], [bs])
                else:
                    act(stg, PS[pb_][:, :], AF.Copy, [bPS[pb_]], [bs])
                if ot < 4:
                    dst, db = DAP(QT, ot * 128 * NT + b * TB, [[NT, 128], [1, TB]]), bQT
                elif ot < 8:
                    dst, db = DAP(KT, (ot - 4) * 128 * 3072 + b * TB, [[3072, 128], [1, TB]]), bKT
                else:
                    dst, db = DAP(UT, (ot - 12) * 128 * NT + b * TB, [[NT, 128], [1, TB]]), bUT
                ld("sp", dst, stg, [bs], [db])
            for t4 in range(4 if SUB >= 3 else 0):
                tok = slice(b * TB + t4 * 128, b * TB + (t4 + 1) * 128)
                for which in ((1, 2) if b == 0 else (2,)):
                    pb_ = 3 + (pi % 2); pi += 1
                    for k in range(8):
                        mm(PS[pb_][:, :], H[:, k, tok], WI[:, k * 2048 + which * 512:k * 2048 + (which + 1) * 512], k == 0, k == 7,
                           [bWI, bH[b]], [bPS[pb_]])
                    if which == 2:
                        vb = VB[vi % 2]; bv = bVB[vi % 2]
                        act(vb, PS[pb_][:, :], AF.Copy, [bPS[pb_]], [bv])
                        if not os.environ.get("TM_NOVS"):
                            ld("sp", DAP(VS, (b * TB + t4 * 128) * 512, [[512, 128], [1, 512]]), vb, [bv], [bVS])
                    if b == 0 and not os.environ.get("TM_NOOUT"):
                        vf = VF[vi % 2]; bf_ = bVF[vi % 2]
                        act(vf, PS[pb_][:, :], AF.Copy, [bPS[pb_]], [bf_])
                        dd = nk if which == 1 else nv
                        if os.environ.get("TM_OUTSCR"):
                            ld("sp", DAP(DBG, (t4 * 128) * 512, [[512, 128], [1, 512]]), vf, [bf_], [])
                        else:
                            out_events.append(ld("sp", DAP(dd, (t4 * 128) * 512, [[512, 128], [1, 512]]), vf, [bf_], []))
                    vi += 1
        for t4 in range(4 if SUB >= 4 else 0):
            vf = VF[vi % 2]; bf_ = bVF[vi % 2]; vi += 1
            ld("sp", vf, DAP(ck, t4 * 128 * 512, [[512, 128], [1, 512]]), [bf_], [bf_])
            for hh in range(4):
                tr(PS[5][:, hh * 128:(hh + 1) * 128], vf[:, hh * 128:(hh + 1) * 128], ident[:], [bf_, bC], [bPS[5]])
            stg = STG[si % 4]; bs = bSTG[si % 4]; si += 1
            act(stg, PS[5][:, :], AF.Copy, [bPS[5]], [bs])
            ld("sp", DAP(KT, 2560 + t4 * 128, [[3072, 128], [128 * 3072, 4], [1, 128]]),
               stg.rearrange("p (h t) -> p h t", t=128), [bs], [bKT])
            vb = VB[vi % 2]; bv = bVB[vi % 2]
            ld("pool", vb, DAP(cv, t4 * 128 * 512, [[512, 128], [1, 512]]), [bv], [bv])
            ld("sp", DAP(VS, (2560 + t4 * 128) * 512, [[512, 128], [1, 512]]), vb, [bv], [bVS])
    P.barrier()
    if ph(4):
        inproj()
    P.barrier()

    def attention():
        KTh = [arb(i * 1536, 1536) for i in range(2)]
        Vh = [arb(3072 + i * 1280, 1280) for i in range(2)]
        Qh = [arb(5632 + i * 1024, 1024) for i in range(2)]
        Qh1 = [arb(12544 + i * 1024, 1024) for i in range(2)]
        PT = [arb(7680 + i * 256, 256) for i in range(4)]
        R1s = [arf(8704, 512), arf(11520, 512)]; R2s = [arf(9216, 512), arf(12032, 512)]
        O1s = [arf(9728, 512), arf(14592, 512)]; O2s = [arf(10240, 512), arf(15104, 512)]
        SQb = arb(10752, 256); AOb = [arb(11008 + i * 256, 256) for i in range(2)]
        bK = [Buf(), Buf()]; bV = [Buf(), Buf()]; bQ = [Buf(), Buf()]; bPT = [Buf() for _ in range(4)]
        bRs = [Buf(), Buf()]; bOs = [Buf(), Buf()]; bSQb = Buf(); bAO = [Buf(), Buf()]
        for i in range(2):
            ms("dve", Qh[i][64:128, :], 0.0, [bQ[i]])
            ms("dve", Qh1[i][0:64, :], 0.0, [bQ[i]])
        heads = []
        for (q0, nq, k0, nkeys) in ((0, 256, 0, 256), (256, 256, 256, 256), (512, 2048, 512, 2560)):
            for h in range(4):
                heads.append((q0, nq, k0, nkeys, h))

        def load_head(hx):
            q0, nq, k0, nkeys, h = heads[hx]
            nkt = nkeys // 128
            p = hx % 2
            ld("sp", KTh[p][:, 0:nkeys], DAP(KT, h * 128 * 3072 + k0, [[3072, 128], [1, nkeys]]), [bKT], [bK[p]])
            ld("sp", Vh[p][:, 0:nkt * 128].rearrange("p (t e) -> p t e", e=128),
               DAP(VS, k0 * 512 + h * 128, [[512, 128], [128 * 512, nkt], [1, 128]]), [bVS], [bV[p]])
            ld("sp", Qh[p][0:64, 0:nq], DAP(QT, h * 128 * NT + q0, [[NT, 64], [1, nq]]), [bQT], [bQ[p]])
            ld("sp", Qh1[p][64:128, 0:nq], DAP(QT, (h * 128 + 64) * NT + q0, [[NT, 64], [1, nq]]), [bQT], [bQ[p]])
        loops = []
        qbi = 0
        for hx, (q0, nq, k0, nkeys, h) in enumerate(heads):
            for qb in range(0, nq, 512):
                for m in range(2):
                    loops.append((hx, qb, min(512, nq - qb), m, nkeys // 128, qbi))
                qbi += 1
        steps = [(li, kt) for li, L in enumerate(loops) for kt in range(L[4])]
        SB_ = (0, 1, 7)
        sbank = {}

        def issue_S(si):
            li, kt = steps[si]
            hx, qb, nqb, m, nkt, _ = loops[li]
            p = hx % 2
            psb = SB_[si % 3]
            sbank[si] = psb
            mm(PS[psb][:, 0:nqb], KTh[p][:, kt * 128:(kt + 1) * 128], (Qh[p] if m == 0 else Qh1[p])[:, qb:qb + nqb],
               True, True, [bK[p], bQ[p]], [bPS[psb]])

        def partA(L):
            hx, qb, nqb, m, nkt, qi = L
            e = qi % 2
            R = (R1s if m == 0 else R2s)[e]; O = (O1s if m == 0 else O2s)[e]
            P.op("dve", lambda e_, n=nqb, R=R, m=m: e_.reciprocal(out=R[:, 0:n], in_=PS[4 + m][:, 0:n]), reads=[bPS[4 + m]], writes=[bRs[e]])
            tt("dve", O[:, 0:nqb], PS[2 + m][:, 0:nqb], R[:, 0:nqb], ALU.mult, [bPS[2 + m], bRs[e]], [bOs[e]])

        aoi = [0]

        def partB(L):
            hx, qb, nqb, m, nkt, qi = L
            q0, nq, k0, nkeys, h = heads[hx]
            e = qi % 2
            O1 = O1s[e]; O2 = O2s[e]; R1 = R1s[e]
            stt("dve", O1[:, 0:nqb], O2[:, 0:nqb], NLAM[:, 0:1], O1[:, 0:nqb], ALU.mult, ALU.add, [bOs[e], bL], [bOs[e]])
            tt("dve", SQb[:, 0:nqb], O1[:, 0:nqb], O1[:, 0:nqb], ALU.mult, [bOs[e]], [bSQb])
            mm(PS[6][:, 0:nqb], onesb[:], SQb[:, 0:nqb], True, True, [bC, bSQb], [bPS[6]])
            act(R1[:, 0:nqb], PS[6][:, 0:nqb], AF.Ln, [bPS[6], bRs[e]], [bRs[e]], scale=1.0 / 128, bias=EPS)
            act(R1[:, 0:nqb], R1[:, 0:nqb], AF.Exp, [bRs[e]], [bRs[e]], scale=-0.5)
            ao = AOb[aoi[0] % 2]; ba = bAO[aoi[0] % 2]; aoi[0] += 1
            stt("dve", ao[:, 0:nqb], O1[:, 0:nqb], SGs[:, 1:2], R1[:, 0:nqb], ALU.mult, ALU.mult, [bOs[e], bRs[e], bL], [ba])
            ld("sp", DAP(MO, h * 128 * NT + q0 + qb, [[NT, 128], [1, nqb]]), ao[:, 0:nqb], [ba], [bMO])
        MW1 = [arb(15616 + i * 2048, 2048) for i in range(2)]; bMW1 = [Buf(), Buf()]
        mod_pending = list(range(12))
        load_head(0)
        issue_S(0)
        if len(steps) > 1:
            issue_S(1)
        deferred = None
        pti = 0
        last_hx = -1
        for si, (li, kt) in enumerate(steps):
            L = loops[li]
            hx, qb, nqb, m, nkt, qi = L
            p = hx % 2
            if hx != last_hx:
                last_hx = hx
                if hx + 1 < len(heads):
                    load_head(hx + 1)
            psb = sbank.pop(si)
            pt = PT[pti % 4]; bp = bPT[pti % 4]; pti += 1
            act(pt[:, 0:nqb], PS[psb][:, 0:nqb], AF.Exp, [bPS[psb]], [bp], scale=0.125)
            if si + 2 < len(steps):
                issue_S(si + 2)
            mm(PS[2 + m][:, 0:nqb], Vh[p][:, kt * 128:(kt + 1) * 128], pt[:, 0:nqb], kt == 0, kt == nkt - 1, [bV[p], bp], [bPS[2 + m]])
            mm(PS[4 + m][:, 0:nqb], onesb[:], pt[:, 0:nqb], kt == 0, kt == nkt - 1, [bC, bp], [bPS[4 + m]])
            if deferred is not None and kt == min(3, nkt - 1):
                partB(deferred)
                deferred = None
            if mod_pending and si % 48 == 40:
                mod_chunk(1, mod_pending.pop(0), MW1, bMW1, bank=6, per_chunk=True)
            if kt == nkt - 1:
                partA(L)
                if m == 1:
                    deferred = L
        if deferred is not None:
            partB(deferred)
        while mod_pending:
            mod_chunk(1, mod_pending.pop(0), MW1, bMW1, bank=6, per_chunk=True)
        mod_finish(1, evac=False)
    if ph(6):
        attention()

    def s5():
        HF = H[:, :, :].rearrange("p k t -> p (k t)").bitcast(F32)
        o = [0]; oh = [0]

        def af(n):
            r = arf(o[0], n); o[0] += n; assert o[0] <= ARN; return r

        def ab(n):
            r = arb(o[0], n); o[0] += n; assert o[0] <= ARN; return r

        def hf(n):
            r = HF[:, oh[0]:oh[0] + n]; oh[0] += n; assert oh[0] <= 10240; return r

        def hb(n):
            r = HF[:, oh[0]:oh[0] + n].bitcast(BF16); oh[0] += n; assert oh[0] <= 10240; return r
        PT_ = af(96)
        XRI = af(64)
        PRE = af((T + 1) * 32); PIM = af((T + 1) * 32)
        FF = af(6 * 32)
        BMR = af(1024); BMI = af(1024)
        CMR = af(1024); CMI = af(1024)
        DV = af(4); DIAG = af(128)
        LAMR = hf(3 * 128)
        TA = hf((T + 1) * 32); TBb = hf((T + 1) * 32)
        oh_bre = oh[0]
        BRE = hf(512); BIM = hf(512); BBR = hf(512); BBI = hf(512)
        CN = hf(512)
        bPrep = Buf("s5prep")
        CN2 = [af(128), af(128)]; bCN2 = [Buf(), Buf()]
        ld("sp", LAMR[0:32, 0:128], DAP(lam_re, 0, [[128, 32], [1, 128]]), [], [bPrep])
        ld("sp", LAMR[0:32, 128:256], DAP(lam_im, 0, [[128, 32], [1, 128]]), [], [bPrep])
        ld("sp", FF[0:32, 0:2], DAP(log_dt, 0, [[2, 32], [1, 2]]), [], [bPrep])
        cp("dve", LAMR[0:32, 256:384].rearrange("p (g q) -> p g q", q=64), FF[0:32, 0:2].unsqueeze(2).to_broadcast([32, 2, 64]), [bPrep], [bPrep])
        for i in range(3):
            tr(PS[0][:, i * 32:(i + 1) * 32], LAMR[0:32, i * 128:(i + 1) * 128], ident[0:32, 0:32], [bPrep, bC], [bPS[0]])
        cp("dve", PT_, PS[0][:, 0:96], [bPS[0]], [bPrep])
        LR = PT_[:, 0:32]; LI = PT_[:, 32:64]; DT = PT_[:, 64:96]
        act(DT, DT, AF.Exp, [bPrep], [bPrep])
        tt("dve", XRI[:, 0:32], LR, DT, ALU.mult, [bPrep], [bPrep])
        tt("dve", XRI[:, 32:64], LI, DT, ALU.mult, [bPrep], [bPrep])
        for tau in range(T + 1):
            ts("dve", PRE[:, tau * 32:(tau + 1) * 32], XRI[:, 0:32], float(tau), None, ALU.mult, None, [bPrep], [bPrep])
            ts("dve", TA[:, tau * 32:(tau + 1) * 32], XRI[:, 32:64], float(tau), None, ALU.mult, None, [bPrep], [bPrep])
        act(PRE, PRE, AF.Exp, [bPrep], [bPrep])
        ts("dve", TBb, TA, 1.0 / (2 * math.pi), 12582912.0, ALU.mult, ALU.add, [bPrep], [bPrep])
        ts("dve", TBb, TBb, -12582912.0, None, ALU.add, None, [bPrep], [bPrep])
        stt("dve", PIM, TA, 1.0 / (2 * math.pi), TBb, ALU.mult, ALU.subtract, [bPrep], [bPrep])
        ts("dve", TA, PIM, 0.25, 12582912.0, ALU.add, ALU.add, [bPrep], [bPrep])
        ts("dve", TA, TA, -12582912.0, None, ALU.add, None, [bPrep], [bPrep])
        stt("dve", TA, PIM, 0.25, TA, ALU.add, ALU.subtract, [bPrep], [bPrep])
        act(TA, TA, AF.Sin, [bPrep], [bPrep], scale=2 * math.pi)
        act(PIM, PIM, AF.Sin, [bPrep], [bPrep], scale=2 * math.pi)
        tt("dve", PIM, PIM, PRE, ALU.mult, [bPrep], [bPrep])
        tt("dve", PRE, TA, PRE, ALU.mult, [bPrep], [bPrep])
        A1R = PRE[:, 32:64]; A1I = PIM[:, 32:64]
        F0, F1, F2, F3, F4, F5 = [FF[:, i * 32:(i + 1) * 32] for i in range(6)]
        tt("dve", F0, LR, LR, ALU.mult, [bPrep], [bPrep])
        tt("dve", F1, LI, LI, ALU.mult, [bPrep], [bPrep])
        tt("dve", F0, F0, F1, ALU.add, [bPrep], [bPrep])
        P.op("dve", lambda e: e.reciprocal(out=F0, in_=F0), reads=[bPrep], writes=[bPrep])
        ts("dve", F1, A1R, -1.0, None, ALU.add, None, [bPrep], [bPrep])
        tt("dve", F2, F1, LR, ALU.mult, [bPrep], [bPrep])
        tt("dve", F3, A1I, LI, ALU.mult, [bPrep], [bPrep])
        tt("dve", F2, F2, F3, ALU.add, [bPrep], [bPrep])
        tt("dve", F2, F2, F0, ALU.mult, [bPrep], [bPrep])
        tt("dve", F3, A1I, LR, ALU.mult, [bPrep], [bPrep])
        tt("dve", F4, F1, LI, ALU.mult, [bPrep], [bPrep])
        tt("dve", F3, F3, F4, ALU.subtract, [bPrep], [bPrep])
        tt("dve", F3, F3, F0, ALU.mult, [bPrep], [bPrep])
        for d_ in range(2):
            ld("sp", BRE.rearrange("p (a d h) -> p a d h", d=2, h=16)[:, :, d_, :], DAP(b_re, d_ * 32768, [[16, 128], [2048, 16], [1, 16]]), [], [bPrep])
            ld("sp", BIM.rearrange("p (a d h) -> p a d h", d=2, h=16)[:, :, d_, :], DAP(b_im, d_ * 32768, [[16, 128], [2048, 16], [1, 16]]), [], [bPrep])
        fo = lambda Fx: bass.AP(Fx.tensor, Fx.offset, [list(Fx.ap[0]), [1, 16], [16, 2], [0, 16]])
        V4 = lambda x: x.rearrange("p (a d h) -> p a d h", d=2, h=16)
        tt("dve", V4(BBR), V4(BRE), fo(F2), ALU.mult, [bPrep], [bPrep])
        tt("dve", V4(BBI), V4(BIM), fo(F3), ALU.mult, [bPrep], [bPrep])
        tt("dve", BBR, BBR, BBI, ALU.subtract, [bPrep], [bPrep])
        tt("dve", V4(BBI), V4(BRE), fo(F3), ALU.mult, [bPrep], [bPrep])
        tt("dve", V4(BRE), V4(BIM), fo(F2), ALU.mult, [bPrep], [bPrep])
        tt("dve", BBI, BBI, BRE, ALU.add, [bPrep], [bPrep])
        ms("pool", BMR, 0.0, [bPrep]); ms("pool", BMI, 0.0, [bPrep])

        def M5(x, p0, p1, g2):
            v = x[p0:p1, :].rearrange("p (a d g h) -> p a d g h", d=2, g=2, h=16)
            return v[:, :, :, g2, :]
        for (src, dst) in ((BBR, BMR), (BBI, BMI)):
            cp("dve", M5(dst, 0, 64, 0), V4(src)[0:64], [bPrep], [bPrep])
            cp("dve", M5(dst, 64, 128, 1), V4(src)[64:128], [bPrep], [bPrep])
        ms("pool", CMR, 0.0, [bPrep]); ms("pool", CMI, 0.0, [bPrep])
        for (csrc, cdst) in ((c_re, CMR), (c_im, CMI)):
            ld("sp", CN.rearrange("p (a q) -> p a q", q=64), DAP(csrc, 0, [[64, 128], [8192, 8], [1, 64]]), [bPrep], [bPrep])
            for dk in range(8):
                pb_ = dk % 2
                cn2 = CN2[dk % 2]
                act(cn2.rearrange("p (g q) -> p g q", q=64), bass.AP(CN.tensor, CN.offset + dk * 64, [list(CN.ap[0]), [0, 2], [1, 64]]), AF.Copy, [bPrep], [bCN2[dk % 2]])
                tr(PS[pb_][:, 0:128], cn2, ident[:], [bCN2[dk % 2], bC], [bPS[pb_]])
                pv = PS[pb_][:, 0:128].rearrange("p (q g h) -> p q g h", g=2, h=16)
                cd = cdst[:, dk * 128:(dk + 1) * 128].rearrange("p (q g h) -> p q g h", g=2, h=16)
                act(cd[0:64, :, 0, :], pv[0:64, :, 0, :], AF.Copy, [bPS[pb_]], [bPrep])
                act(cd[64:128, :, 1, :], pv[64:128, :, 1, :], AF.Copy, [bPS[pb_]], [bPrep])
        ld("sp", DV, DAP(s5_d, 0, [[1, 128], [128, 4]]), [], [bPrep], **SL)

        NSL = NCL + 1; NSC = NCC // 2 + 1
        ZL = hb(32 * NSL)
        ZC = hb(32 * 2 * NSC)
        ZLv = ZL.rearrange("p (r c s) -> p r c s", r=2, c=32)
        ZCv = ZC.rearrange("p (r c b s) -> p r c b s", r=2, c=32, b=2)
        CA = hf(64); CB = hf(64); STI = hf(64)
        RSL = [hf(64) for _ in range(2)]
        RSC = [hf(128) for _ in range(2)]
        S1 = hf(128); S2 = hf(128); FIN = hf(128)
        bZ = Buf("zs")
        pw = lambda X: bass.AP(X.tensor, X.offset + T * 32, [list(X.ap[0]), [1, 16], [16, 2]])
        c3 = lambda X, h: X[:, h * 32:(h + 1) * 32].rearrange("p (a d) -> p a d", d=2)
        cp("dve", c3(CA, 0), pw(PRE), [bPrep], [bZ]); cp("dve", c3(CA, 1), pw(PRE), [bPrep], [bZ])
        ts("dve", c3(CB, 0), pw(PIM), -1.0, None, ALU.mult, None, [bPrep], [bZ]); cp("dve", c3(CB, 1), pw(PIM), [bPrep], [bZ])
        for d_ in range(2):
            ld("sp", STI.rearrange("p (a d r) -> p a d r", d=2, r=2)[:, :, d_, :], DAP(st, d_ * 4096, [[2, 128], [256, 16], [1, 2]]), [], [bZ])
        for r in range(2):
            cp("dve", RSL[0][:, r * 32:(r + 1) * 32].rearrange("p (a d) -> p a d", d=2),
               STI.rearrange("p (a d r) -> p a d r", d=2, r=2)[:, :, :, r], [bZ], [bZ])
        cp("dve", ZLv[:, :, :, 0], RSL[0].rearrange("p (r c) -> p r c", r=2), [bZ], [bZ])
        ms("pool", RSC[0], 0.0, [bZ])
        ms("pool", ZCv[:, :, :, :, 0], 0.0, [bZ])

        TABS = [ab(4096), ab(4096)]
        TZS = [ab(2048), HF[:, oh_bre:oh_bre + 2048].bitcast(BF16)]
        GEN = [af(512) for _ in range(2)]
        o_g = o[0]
        G1 = af(256); G2 = af(256)
        UK = ab(1280)
        bTABS = [Buf(), Buf()]; bTZS = [Buf(), Buf()]; bGEN = [Buf(), Buf()]; bG = Buf(); bUK = Buf()
        bTZS[1].w = bPrep.w; bTZS[1].r = list(bPrep.r)
        tabvs = [t_.rearrange("p (d t r c) -> p d t r c", d=2, t=T, r=2) for t_ in TABS]
        tzvs = [t_.rearrange("p (t c) -> p t c", c=128) for t_ in TZS]

        def powv(X, tau, k):
            return bass.AP(X.tensor, X.offset + tau * 32 + k * 4, [list(X.ap[0]), [16, 2], [1, 4], [0, 32]])

        def bmv(X, k):
            return bass.AP(X.tensor, X.offset + k * 4 * 64, [list(X.ap[0]), [32, 2], [64, 4], [1, 32]])

        def cmv(X, k):
            return bass.AP(X.tensor, X.offset + k * 128, [list(X.ap[0]), [512, 2], [32, 4], [1, 32]])
        g4 = lambda X: X.rearrange("p (d q c) -> p d q c", d=2, q=4)
        gi = [0]

        def cgen(XR_, XI_, tau, k, negim):
            gt_ = GEN[gi[0] % 2]; bg = bGEN[gi[0] % 2]; gi[0] += 1
            re = g4(gt_[:, 0:256]); im = g4(gt_[:, 256:512])
            tt("dve", re, XR_, powv(PRE, tau, k), ALU.mult, [bPrep], [bg])
            tt("dve", g4(G2), XI_, powv(PIM, tau, k), ALU.mult, [bPrep], [bG])
            tt("dve", re, re, g4(G2), ALU.subtract, [bG], [bg])
            tt("dve", im, XR_, powv(PIM, tau, k), ALU.mult, [bPrep], [bg])
            tt("dve", g4(G1), XI_, powv(PRE, tau, k), ALU.mult, [bPrep], [bG])
            if negim:
                stt("dve", im, im, -1.0, g4(G1), ALU.mult, ALU.subtract, [bG], [bg])
            else:
                tt("dve", im, im, g4(G1), ALU.add, [bG], [bg])
            return gt_, bg

        zi = [0]

        def gen1(k, tau):
            tabv = tabvs[k % 2]; bTAB = bTABS[k % 2]
            gt_, bg = cgen(bmv(BMR, k), bmv(BMI, k), tau, k, False)
            for r in range(2):
                for d in range(2):
                    idx = r * 2 + d
                    tr(PS[idx % 2][:, (idx // 2) * 128:(idx // 2 + 1) * 128],
                       gt_[:, r * 256 + d * 128:r * 256 + (d + 1) * 128], ident[:], [bg, bC], [bPS[idx % 2]])
            for r in range(2):
                for d in range(2):
                    idx = r * 2 + d
                    src = PS[idx % 2][:, (idx // 2) * 128:(idx // 2 + 1) * 128]
                    act(tabv[:, d, tau, r, :], src, AF.Copy, [bPS[idx % 2]], [bTAB])

        def zmm4(k, d, r):
            tabv = tabvs[k % 2]; bTAB = bTABS[k % 2]
            for (u0, n, c0) in ((512, NCL, 0), (0, NCC, 128)):
                for i in range(T):
                    tau = (T - 1 - i) if d == 0 else i
                    for q in range(4):
                        lhs = tabv[32 * q:32 * q + 32, d, tau, r, :]
                        mm(PS[2 + q][:, c0:c0 + n], lhs, AP(UK, u0 + i, [[T, n]], 32 * q, 32 * q + 32), i == 0, i == T - 1,
                           [bTAB, bUK], [bPS[2 + q]], tile_position=(32 * q, 0))

        def zev4(k, d, r):
            for q in range(4):
                col = k * 8 + q * 2 + d
                pb_ = 2 + q
                pc = PS[pb_][:, 128:128 + NCC].rearrange("p (b s) -> p b s", b=2)
                if d == 0:
                    act(ZLv[:, r, col, 1:NSL], PS[pb_][:, 0:NCL], AF.Copy, [bPS[pb_]], [bZ])
                    act(ZCv[:, r, col, :, 1:NSC], pc, AF.Copy, [bPS[pb_]], [bZ])
                else:
                    act(ZLv[:, r, col, NSL - 1:0:-1], PS[pb_][:, 0:NCL], AF.Copy, [bPS[pb_]], [bZ])
                    act(ZCv[:, r, col, :, NSC - 1:0:-1], pc, AF.Copy, [bPS[pb_]], [bZ])
        for tau in range(T):
            gen1(0, tau)
        for k in range(4):
            ld("sp", UK, DAP(UT, k * 128 * NT, [[NT, 128], [1, NT]]), [bUT, bUK], [bUK])
            for g_ in range(4):
                d, r = g_ // 2, g_ % 2
                zmm4(k, d, r)
                if k + 1 < 4:
                    for t_ in range(4):
                        gen1(k + 1, g_ * 4 + t_)
                zev4(k, d, r)
        r3 = lambda X: X.rearrange("p (r c) -> p r c", r=2)
        bS = Buf("scanscratch"); bS2 = Buf("scanscratch2")
        bRSL = [Buf(), Buf()]; bRSC = [Buf(), Buf()]
        bZHL = [Buf(), Buf()]; bZHC = [Buf(), Buf()]
        for bb in bRSL + bRSC + bZHL + bZHC:
            bb.w = bZ.w; bb.r = list(bZ.r)
        for s in range(NCL):
            prev = RSL[s % 2]; nxt = RSL[(s + 1) % 2]
            bp_ = bRSL[s % 2]; bn_ = bRSL[(s + 1) % 2]; bh_ = bZHL[(s + 1) % 2]
            sw = bass.AP(prev.tensor, prev.offset + 32, [list(prev.ap[0]), [-32, 2], [1, 32]])
            tt("dve", r3(S1[:, 0:64]), r3(prev), r3(CA), ALU.mult, [bp_], [bS])
            tt("pool", r3(S2[:, 0:64]), sw, r3(CB), ALU.mult, [bp_], [bS2])
            tt("dve", r3(nxt), ZLv[:, :, :, s + 1], r3(S1[:, 0:64]), ALU.add, [bh_, bS], [bn_])
            tt("dve", r3(nxt), r3(nxt), r3(S2[:, 0:64]), ALU.add, [bS2], [bn_])
            act(ZLv[:, :, :, s + 1], r3(nxt), AF.Copy, [bn_], [bh_])
        r4 = lambda X: X.rearrange("p (r c b) -> p r c b", r=2, c=32)
        c4 = lambda X: X.rearrange("p (r c) -> p r c", r=2).unsqueeze(3).to_broadcast([128, 2, 32, 2])
        for s in range(NSC - 1):
            prev = RSC[s % 2]; nxt = RSC[(s + 1) % 2]
            bp_ = bRSC[s % 2]; bn_ = bRSC[(s + 1) % 2]; bh_ = bZHC[(s + 1) % 2]
            sw = bass.AP(prev.tensor, prev.offset + 64, [list(prev.ap[0]), [-64, 2], [2, 32], [1, 2]])
            tt("dve", r4(S1), r4(prev), c4(CA), ALU.mult, [bp_, bS], [bS])
            tt("pool", r4(S2), sw, c4(CB), ALU.mult, [bp_], [bS2])
            tt("dve", r4(nxt), ZCv[:, :, :, :, s + 1], r4(S1), ALU.add, [bh_, bS], [bn_])
            tt("dve", r4(nxt), r4(nxt), r4(S2), ALU.add, [bS2], [bn_])
            act(ZCv[:, :, :, :, s + 1], r4(nxt), AF.Copy, [bn_], [bh_])
        fin = RSC[(NSC - 1) % 2]
        fv = FIN.rearrange("p (b d a r) -> p b d a r", b=2, d=2, a=16)
        alldeps = bRSL + bRSC + bZHL + bZHC + [bZ]
        for r in range(2):
            for b_ in range(2):
                src = bass.AP(fin.tensor, fin.offset + r * 64 + b_, [list(fin.ap[0]), [2, 2], [4, 16]])
                cp("dve", fv[:, b_, :, :, r], src, alldeps, [bZ])
        out_events.append(ld("sp", DAP(ns, 0, [[2, 128], [256, 64], [1, 2]]), FIN.rearrange("p (x r) -> p x r", r=2), [bZ], []))

        _ys = af(512); YS = [_ys, _ys]; _by = Buf(); bYS = [_by, _by]
        BMK = [[af(128), af(128)], [af(128), af(128)]]; bBMK = Buf()
        _gs = ab(256); GS_ = [_gs, _gs]; _bg = Buf(); bGS_ = [_bg, _bg]
        Y1 = arf(o_g, 512); bY1 = bG
        yi = [0]

        def prep2(k):
            ts("dve", DIAG, ident[:], DV[:, k:k + 1], None, ALU.mult, None, [bC, bPrep], [bPrep])
            for d in range(2):
                cp("dve", BMK[0][d].rearrange("p (q c) -> p q c", c=32), bmv(BMR, k)[:, d, :, :], [bPrep], [bBMK])
                cp("dve", BMK[1][d].rearrange("p (q c) -> p q c", c=32), bmv(BMI, k)[:, d, :, :], [bPrep], [bBMK])

        def gen2(k, tau):
            tabv = tabvs[k % 2]; bTAB = bTABS[k % 2]; tzv = tzvs[k % 2]; bTZ = bTZS[k % 2]
            gt_, bg = cgen(cmv(CMR, k), cmv(CMI, k), tau, k, True)
            if tau >= 1:
                for r in range(2):
                    act(tabv[:, 0, tau - 1, r, :], gt_[:, r * 256:r * 256 + 128], AF.Copy, [bg], [bTAB])
                    act(tabv[:, 1, T - tau, r, :], gt_[:, r * 256 + 128:r * 256 + 256], AF.Copy, [bg], [bTAB])
            if tau <= T - 1:
                for d in range(2):
                    mm(PS[d][:, 0:128], BMK[0][d], gt_[:, d * 128:(d + 1) * 128], True, False, [bBMK, bg], [bPS[d]])
                    mm(PS[d][:, 0:128], BMK[1][d], gt_[:, 256 + d * 128:256 + (d + 1) * 128], False, True, [bBMK, bg], [bPS[d]])
                if tau == 0:
                    tt("dve", G1[:, 0:128], PS[0][:, 0:128], maskq[:], ALU.mult, [bPS[0], bC], [bG])
                    tt("dve", G2[:, 0:128], PS[1][:, 0:128], maskq[:], ALU.mult, [bPS[1], bC], [bG])
                    tt("dve", G1[:, 0:128], G1[:, 0:128], G2[:, 0:128], ALU.add, [bG], [bG])
                    tt("dve", tzv[:, T - 1, :], G1[:, 0:128], DIAG, ALU.add, [bG, bPrep], [bTZ])
                else:
                    tt("dve", tzv[:, T - 1 + tau, :], PS[0][:, 0:128], maskq[:], ALU.mult, [bPS[0], bC], [bTZ])
                    tt("dve", tzv[:, T - 1 - tau, :], PS[1][:, 0:128], maskq[:], ALU.mult, [bPS[1], bC], [bTZ])

        def outj(k, j):
            tabv = tabvs[k % 2]; bTAB = bTABS[k % 2]; tzv = tzvs[k % 2]; bTZ = bTZS[k % 2]
            outl = PS[2 + j // 4][:, (j % 4) * 128:(j % 4 + 1) * 128]; bol = bPS[2 + j // 4]
            outc = PS[6][:, j * 32:(j + 1) * 32]; boc = bPS[6]
            for i in range(T):
                mm(outl, tzv[:, T - 1 + (j - i), :], AP(UK, 512 + i, [[T, NCL]]), i == 0, False, [bTZ, bUK], [bol])
            for i in range(T):
                mm(outc, tzv[:, T - 1 + (j - i), :], AP(UK, i, [[T, NCC]]), i == 0, False, [bTZ, bUK], [boc])
            nn = NSC - 1
            for d in range(2):
                for r in range(2):
                    for q in range(4):
                        col = k * 8 + q * 2 + d
                        lhs = tabv[:, d, j, r, 32 * q:32 * q + 32]
                        base = (r * 32 + col) * NSL
                        rhs = AP(ZL, base, [[1, NCL]]) if d == 0 else AP(ZL, base + NCL - 1, [[-1, NCL]])
                        mm(outl[32 * q:32 * q + 32], lhs, rhs, False, (d == 1 and r == 1), [bTAB, bZ], [bol], tile_position=(0, 32 * q))
            for d in range(2):
                for r in range(2):
                    for b_ in range(2):
                        for q in range(4):
                            col = k * 8 + q * 2 + d
                            lhs = tabv[:, d, j, r, 32 * q:32 * q + 32]
                            base = (r * 32 + col) * 2 * NSC + b_ * NSC
                            rhs = AP(ZC, base, [[1, nn]]) if d == 0 else AP(ZC, base + nn - 1, [[-1, nn]])
                            mm(outc[32 * q:32 * q + 32, b_ * nn:(b_ + 1) * nn], lhs, rhs, False, (d == 1 and r == 1 and b_ == 1), [bTAB, bZ], [boc], tile_position=(0, 32 * q))

        def evac(k):
            for blk in range(5):
                ys = YS[yi[0] % 2]; by = bYS[yi[0] % 2]; gs_ = GS_[yi[0] % 2]; bgs = bGS_[yi[0] % 2]; yi[0] += 1
                if blk == 0:
                    src = PS[6][:, :].rearrange("p (j c) -> p c j", j=T)
                    cp("dve", ys.rearrange("p (c j) -> p c j", j=T), src, [bPS[6]], [by])
                else:
                    c0 = (blk - 1) * 32
                    for jb in range(4):
                        src = PS[2 + jb][:, :].rearrange("p (j c) -> p c j", j=4)[:, c0:c0 + 32, :]
                        dst = ys.rearrange("p (c j) -> p c j", j=T)[:, :, jb * 4:(jb + 1) * 4]
                        if jb % 2 == 0:
                            cp("dve", dst, src, [bPS[2 + jb]], [by])
                        else:
                            act(dst, src, AF.Copy, [bPS[2 + jb]], [by])
                act(Y1, ys, AF.Square, [by], [bY1])
                ts("pool", Y1, Y1, 0.044715, 1.0, ALU.mult, ALU.add, [bY1], [bY1])
                tt("pool", Y1, Y1, ys, ALU.mult, [bY1, by], [bY1])
                act(Y1, Y1, AF.Sigmoid, [bY1], [bY1], scale=GC)
                tt("dve", gs_, Y1, ys, ALU.mult, [bY1, by], [bgs])
                ld("sp", DAP(GT, k * 128 * NT + blk * TB, [[NT, 128], [1, TB]]), gs_, [bgs], [bGT])
        prep2(0)
        for tau in range(T + 1):
            gen2(0, tau)
        for k in range(4):
            ld("sp", UK, DAP(UT, k * 128 * NT, [[NT, 128], [1, NT]]), [bUT, bUK], [bUK])
            if k + 1 < 4:
                prep2(k + 1)
            for j in range(T):
                if k + 1 < 4:
                    gen2(k + 1, j)
                    if j == T - 1:
                        gen2(k + 1, T)
                outj(k, j)
            evac(k)
    P.barrier()
    if ph(7):
        s5()
    P.barrier()

    def glu():
        WG = arb(0, 1024)
        GB = [arb(1024 + i * 1024, 1024) for i in range(2)]
        SG_ = [arf(3072 + i * 512, 512) for i in range(2)]
        OB = [arb(4096 + i * 256, 256) for i in range(2)]
        bWG = Buf(); bGB = [Buf(), Buf()]; bSG = [Buf(), Buf()]; bOB = [Buf(), Buf()]
        ld("pool", WG.rearrange("p (k c) -> p k c", c=512), DAP(w_glu, 0, [[512, 128], [128 * 512, 4], [1, 512]]), [], [bWG])
        oi = 0
        for b in range(NB):
            gb = GB[b % 2]; bg = bGB[b % 2]
            ld("sp", gb.rearrange("p (k t) -> p k t", t=512), DAP(GT, b * TB, [[NT, 128], [128 * NT, 4], [1, TB]]), [bGT], [bg])
            for m in range(4):
                pb_ = oi % 2
                for k in range(4):
                    mm(PS[pb_][:, :], WG[:, k * 512 + m * 128:k * 512 + (m + 1) * 128], gb[:, k * 512:(k + 1) * 512], k == 0, k == 3, [bWG, bg], [bPS[pb_]])
                sg = SG_[oi % 2]; bs = bSG[oi % 2]; ob = OB[oi % 2]; bo = bOB[oi % 2]; oi += 1
                act(sg, PS[pb_][:, :], AF.Sigmoid, [bPS[pb_]], [bs])
                tt("dve", ob, sg, gb[:, m * 512:(m + 1) * 512], ALU.mult, [bs, bg], [bo])
                ld("sp", DAP(MO, (4 + m) * 128 * NT + b * TB, [[NT, 128], [1, TB]]), ob, [bo], [bMOb[b]])
    if ph(8):
        glu()

    def outproj():
        WO = arb(4608, 4096)
        MB_ = [arb(8704 + i * 2048, 2048) for i in range(2)]
        bWO = Buf(); bMB_ = [Buf(), Buf()]
        ld("pool", WO.rearrange("p (k c) -> p k c", c=D), DAP(w_out, 0, [[D, 128], [128 * D, 8], [1, D]]), [], [bWO])
        for b in range(NB):
            s = 0 if b == 0 else 1
            tb = slice(b * TB, (b + 1) * TB)
            mb = MB_[b % 2]; bm = bMB_[b % 2]
            ld("sp", mb.rearrange("p (k t) -> p k t", t=512), DAP(MO, b * TB, [[NT, 128], [128 * NT, 8], [1, TB]]), [bMO, bMOb[b]], [bm])
            for m in range(8):
                pb_ = 2 + m % 4
                for k in range(8):
                    mm(PS[pb_][:, :], WO[:, k * D + m * 128:k * D + (m + 1) * 128], mb[:, k * 512:(k + 1) * 512], k == 0, k == 7, [bWO, bm], [bPS[pb_]])
                stt("dve", XR[:, m, tb], PS[pb_][:, :], GATE[:, 0, 0, m, s:s + 1], XR[:, m, tb], ALU.mult, ALU.add,
                    [bPS[pb_], bMODs[0], bXR[b]], [bXR[b]])
    if ph(9):
        outproj()
    if ph(10):
        norm_to_H(0, 1)
    P.barrier()

    if ph(10):
        mlp(0)

    if ph(11):
        norm_to_H(1, 0)
    P.barrier()

    def poolmix():
        LP = 2048 + 16
        PA = arf(0, LP); PB = arf(LP, LP); RCl = arf(2 * LP, 2048)
        o0 = 2 * LP + 2048
        RCc2 = arf(o0, 512); TMPP = [arf(o0 + 512, 512), arf(o0 + 1024, 512)]
        PWt = arb(o0 + 1536, 256)
        PSC = arf(o0 + 1792, 16)
        bPA = Buf(); bPB = Buf(); bRC = Buf(); bPW = Buf(); bPSC = Buf(); bTMPP = [Buf(), Buf()]
        ld("sp", PSC[:, 0:8], DAP(pool_scale, 0, [[1, 128], [128, 8]]), [], [bPSC], **SL)
        gp2 = nc.alloc_sbuf_tensor("GP2", [128, 8, 2], F32)
        tt("dve", gp2[:], GATE[:, 1, 0, :, :], PSC[:, 0:8].unsqueeze(2).to_broadcast([128, 8, 2]), ALU.mult, [bMODs[1], bPSC], [bPSC])
        ts("dve", gp2[:], gp2[:], -1.0, None, ALU.mult, None, [bPSC], [bPSC])

        def wsum(g, L):
            Lp = L + 16
            cur, bc, oth, bo = PA, bPA, PB, bPB
            tt("dve", oth[:, 1:Lp], cur[:, 0:Lp - 1], cur[:, 1:Lp], ALU.add, [bc], [bo])
            cur, bc, oth, bo = oth, bo, cur, bc
            lo, hi, sh = 1, Lp, 1
            for _ in range(g):
                nlo, nhi = lo + sh, hi - sh
                tt("dve", oth[:, nlo:nhi], cur[:, nlo - sh:nhi - sh], cur[:, nlo + sh:nhi + sh], ALU.add, [bc], [bo])
                cur, bc, oth, bo = oth, bo, cur, bc
                lo, hi, sh = nlo, nhi, sh * 2
            return cur, bc
        pc = [0]; ti_ = [0]
        for g in range(4):
            w = 2 ** (g + 1); half = w // 2
            ld("pool", PWt.rearrange("p (k c) -> p k c", c=256), DAP(pool_w, g * 65536, [[256, 128], [128 * 256, 2], [1, 256]]), [bPW], [bPW])
            L = 256
            ms("dve", PA[:, 0:L + 16], 0.0, [bPA]); ms("dve", PA[:, 8:8 + L], 1.0, [bPA])
            res, br = wsum(g, L)
            P.op("dve", lambda e, res=res: e.reciprocal(out=RCc2[:, 0:256], in_=res[:, 8:8 + 256]), reads=[br], writes=[bRC])
            cp("dve", RCc2[:, 256:512], RCc2[:, 0:256], [bRC], [bRC])
            cp("dve", RCl[:, 0:2048].rearrange("p (a c) -> p a c", c=16),
               RCc2[:, 120:136].unsqueeze(1).to_broadcast([128, 128, 16]), [bRC], [bRC])
            cp("dve", RCl[:, 0:8], RCc2[:, 0:8], [bRC], [bRC])
            cp("dve", RCl[:, 2040:2048], RCc2[:, 248:256], [bRC], [bRC])
            for b in range(NB):
                s_ = 0 if b == 0 else 1
                t0 = b * TB
                tb = slice(t0, t0 + TB)
                if b == 0:
                    segs = ((0, 256, True, True), (256, 512, True, True))
                    rc = RCc2[:, 0:512]
                else:
                    segs = ((0, 512, b == 1, b == NB - 1),)
                    rc = RCl[:, (b - 1) * TB:b * TB]
                hdeps = [bH[x] for x in (b - 1, b, b + 1) if 0 <= x < NB]
                for mm_ in range(2):
                    m = 2 * g + mm_
                    pa_ = (2 * pc[0]) % 6; pb_ = (2 * pc[0] + 1) % 6; pc[0] += 1
                    lw = [PWt[:, kk * 256 + mm_ * 128:kk * 256 + (mm_ + 1) * 128] for kk in range(2)]
                    jobs = [(0, 0, 512)]
                    for sft in list(range(-half, 0)) + list(range(1, half)):
                        for (lo, hi, le, re_) in segs:
                            c0 = lo + (-sft if (sft < 0 and le) else 0)
                            c1 = hi - (sft if (sft > 0 and re_) else 0)
                            jobs.append((sft, c0, c1))
                    nj = len(jobs)
                    for ji, (sft, c0, c1) in enumerate(jobs):
                        for kk in range(2):
                            mm(PS[pa_][:, c0:c1], lw[kk], H[:, 2 * g + kk, t0 + c0 + sft:t0 + c1 + sft],
                               ji == 0 and kk == 0, ji == nj - 1 and kk == 1, [bPW] + hdeps, [bPS[pa_]])
                    for kk in range(2):
                        mm(PS[pb_][:, :], lw[kk], H[:, 2 * g + kk, tb], kk == 0, kk == 1, [bPW, bH[b]], [bPS[pb_]])
                    tp = TMPP[ti_[0] % 2]; btp = bTMPP[ti_[0] % 2]; ti_[0] += 1
                    tt("dve", tp, PS[pa_][:, :], rc, ALU.mult, [bPS[pa_], bRC], [btp])
                    tt("dve", tp, PS[pb_][:, :], tp, ALU.subtract, [bPS[pb_], btp], [btp])
                    stt("dve", XR[:, m, tb], tp, gp2[:, m, s_:s_ + 1], XR[:, m, tb], ALU.mult, ALU.add, [btp, bPSC, bXR[b]], [bXR[b]])
    if ph(11):
        poolmix()
    if ph(12):
        norm_to_H(1, 1)
    P.barrier()

    if ph(12):
        mlp(1)

    def final():
        YB = arf(10752, 4096)
        YT = [arf(14848 + i * 1024, 1024) for i in range(2)]
        bYB = Buf(); bYT = [Buf(), Buf()]
        ti = 0
        for b in range(NB):
            norm_block(b, lambda k: NG[:, 4, k:k + 1], lambda k: None, lambda k: YB[:, k * 512:(k + 1) * 512], [bYB])
            for t4 in range(4):
                yt = YT[ti % 2]; by = bYT[ti % 2]
                pa, pb2 = (0, 1) if ti % 2 == 0 else (2, 3)
                ti += 1
                for k in range(8):
                    pp = pa if k < 4 else pb2
                    tr(PS[pp][:, (k % 4) * 128:(k % 4 + 1) * 128], YB[:, k * 512 + t4 * 128:k * 512 + (t4 + 1) * 128], ident[:], [bYB, bC], [bPS[pp]])
                cp("dve", yt[:, 0:512], PS[pa][:, :], [bPS[pa]], [by])
                act(yt[:, 512:1024], PS[pb2][:, :], AF.Copy, [bPS[pb2]], [by])
                tglob = b * 4 + t4
                if tglob < 4:
                    dst = DAP(yc, tglob * 128 * D, [[D, 128], [1, D]])
                else:
                    dst = DAP(yl, (tglob - 4) * 128 * D, [[D, 128], [1, D]])
                out_events.append(ld("sp", dst, yt, [by], []))
    if ph(13):
        final()

    P.wait_all_on("sp", out_events)
    P.emit()
    return nc


_NC = None
_DEBUG_HOOK = None


def _consts():
    ident = np.eye(128, dtype=np.float32)
    maskq = np.kron(np.eye(4, dtype=np.float32), np.ones((32, 32), np.float32))
    t = np.arange(2048)
    posr = (t // 64).astype(np.float32)
    posc = (t % 64).astype(np.float32)
    fi = np.full((128, 2), 1.0e4, np.float32)
    for p in range(128):
        dd = p % 64
        if dd < 32:
            fi[p, 0] = dd % 16
        else:
            fi[p, 1] = (dd - 32) % 16
    return dict(c_ident=ident, c_maskq=maskq, c_posr=posr, c_posc=posc, c_fi=fi)


def kernel(x_prompt, x_sample, cache_k, cache_v, state_s5, c, c_ctx, mod_w, mod_b, norm_g,
           mix_w_in, mix_w_out, diff_lambda_qk, diff_subln_g, s5_lambda_re, s5_lambda_im, s5_log_dt,
           s5_b_re, s5_b_im, s5_c_re, s5_c_im, s5_d, s5_w_glu, pool_w, pool_scale, ff_w1, ff_w2,
           final_norm_g):
    global _NC
    f = lambda a: np.ascontiguousarray(np.asarray(a, dtype=np.float32))
    if _NC is None:
        _NC = build()
    nc = _NC
    cst = _consts()
    shared = dict(mod_w=f(mod_w), mod_b=f(mod_b), norm_g=f(norm_g), w_in=f(mix_w_in)[0], w_out=f(mix_w_out)[0],
                  lam_qk=f(diff_lambda_qk)[0], subln=f(diff_subln_g)[0], lam_re=f(s5_lambda_re)[0], lam_im=f(s5_lambda_im)[0],
                  log_dt=f(s5_log_dt)[0], b_re=f(s5_b_re)[0], b_im=f(s5_b_im)[0], c_re=f(s5_c_re)[0], c_im=f(s5_c_im)[0],
                  s5_d=f(s5_d)[0], w_glu=f(s5_w_glu)[0], pool_w=f(pool_w)[0], pool_scale=f(pool_scale)[0],
                  ff_w1=f(ff_w1), ff_w2=f(ff_w2), fng=f(final_norm_g), **cst)
    xp = f(x_prompt); xs = f(x_sample); ckk = f(cache_k); cvv = f(cache_v); sst = f(state_s5); cc = f(c); cctx = f(c_ctx)
    in_maps = []
    for core in range(8):
        ls = core // 2
        m = dict(shared)
        m["xc"] = np.ascontiguousarray(xp[2 * core:2 * core + 2].reshape(512, D))
        m["xl"] = np.ascontiguousarray(xs[ls])
        m["ck"] = np.ascontiguousarray(ckk[ls, 0].reshape(512, 512))
        m["cv"] = np.ascontiguousarray(cvv[ls, 0].reshape(512, 512))
        m["st"] = np.ascontiguousarray(sst[ls, 0])
        m["cvec"] = np.ascontiguousarray(np.stack([cctx, cc[ls]], axis=0))
        in_maps.append(m)
    if _DEBUG_HOOK is not None:
        return _DEBUG_HOOK(nc, in_maps)
    res = run_bass_kernel_spmd(nc, in_maps, core_ids=list(range(8)))
    R = res.results
    y_prompt = np.concatenate([R[i]["yc"].reshape(2, 256, D) for i in range(8)], axis=0)
    y_sample = np.stack([np.concatenate([R[2 * s]["yl"][:1024], R[2 * s + 1]["yl"][1024:]], axis=0) for s in range(4)], axis=0)
    nk = np.concatenate([R[i]["nk"].reshape(2, 1, 256, 4, 128) for i in range(8)], axis=0)
    nv = np.concatenate([R[i]["nv"].reshape(2, 1, 256, 4, 128) for i in range(8)], axis=0)
    nst = np.concatenate([R[i]["ns"].reshape(2, 1, 2, 32, 64, 2) for i in range(8)], axis=0)
    return (y_prompt.astype(np.float32), y_sample.astype(np.float32), nk.astype(np.float32),
            nv.astype(np.float32), nst.astype(np.float32))
```
